# Optimizing a Trainium2 kernel written in Bass

```python
import jax, jax.numpy as jnp
from jax import lax
import numpy as np

D_MODEL = 1024
BATCH = 8
SEQ = 2048
DEPTH = 4

N_MIXERS = 2
N_A_LAYERS = (DEPTH + 1) // 2
N_B_LAYERS = DEPTH // 2

DN_HEADS = 8
DN_HEAD_DIM = 128
DN_KEY = DN_HEADS * DN_HEAD_DIM
DN_VAL = DN_HEADS * DN_HEAD_DIM
DN_QKV = 2 * DN_KEY + DN_VAL
DN_IN = DN_QKV + DN_VAL + 2 * DN_HEADS
DN_CONV = 4
DN_CHUNK = 64

CF_CH = D_MODEL
CF_KERNEL = 31

FF_DIM = 4 * D_MODEL

ALPHA = (2.0 * DEPTH) ** 0.25
BETA_INIT = (8.0 * DEPTH) ** -0.25
N_MOD = 6
LN_EPS = 1e-5
RMS_EPS = 1e-6
L2_EPS = 1e-6

kernel_name = "hybrid_gdn_conformer_deepnorm_adaln"


def layer_norm(x, g, b):
    xf = x.astype(jnp.float32)
    mu = jnp.mean(xf, axis=-1, keepdims=True)
    var = jnp.mean(jnp.square(xf - mu), axis=-1, keepdims=True)
    return ((xf - mu) * lax.rsqrt(var + LN_EPS) * g + b).astype(x.dtype)


def l2_normalize(x):
    xf = x.astype(jnp.float32)
    return xf * lax.rsqrt(jnp.sum(xf * xf, axis=-1, keepdims=True) + L2_EPS)


def causal_depthwise_conv(x, w):
    k, ch = w.shape
    return lax.conv_general_dilated(
        x, w.astype(x.dtype)[:, None, :], window_strides=(1,), padding=((k - 1, 0),),
        dimension_numbers=('NWC', 'WIO', 'NWC'), feature_group_count=ch)


def gated_delta_rule(q, k, v, g, beta):
    b, s, h, dk = q.shape
    dv = v.shape[-1]
    c = DN_CHUNK
    n = s // c
    f32 = jnp.float32

    def chunks(t):
        t = t.astype(f32).reshape((b, n, c, h) + t.shape[3:])
        return jnp.moveaxis(t, 3, 1)

    q, k, v, g, beta = chunks(q), chunks(k), chunks(v), chunks(g), chunks(beta)
    q = q * (dk ** -0.5)
    gam = jnp.cumsum(g, axis=-1)
    causal = jnp.tril(jnp.ones((c, c), dtype=bool))
    strict = jnp.tril(jnp.ones((c, c), dtype=bool), -1)
    diff = gam[..., :, None] - gam[..., None, :]
    decay = jnp.exp(jnp.where(causal, diff, -jnp.inf))

    kb = k * beta[..., None]
    a_kk = jnp.where(strict, jnp.einsum('bhncd,bhnsd->bhncs', kb, k) * decay, 0.0)
    eye = jnp.eye(c, dtype=f32)
    t_inv = lax.linalg.triangular_solve(eye + a_kk, jnp.broadcast_to(eye, a_kk.shape),
                                        left_side=True, lower=True)
    u = jnp.matmul(t_inv, v * beta[..., None])
    w = jnp.matmul(t_inv, kb * jnp.exp(gam)[..., None])
    a_qk = jnp.einsum('bhncd,bhnsd->bhncs', q, k) * decay
    q_dec = q * jnp.exp(gam)[..., None]
    k_dec = k * jnp.exp(gam[..., -1:] - gam)[..., None]
    g_last = jnp.exp(gam[..., -1])

    def step(state, xs):
        u_i, w_i, a_i, qd_i, kd_i, gl_i = xs
        v_new = u_i - jnp.einsum('bhck,bhkv->bhcv', w_i, state)
        o_i = (jnp.einsum('bhck,bhkv->bhcv', qd_i, state)
               + jnp.einsum('bhcs,bhsv->bhcv', a_i, v_new))
        state = state * gl_i[..., None, None] + jnp.einsum('bhck,bhcv->bhkv', kd_i, v_new)
        return state, o_i

    xs = tuple(jnp.moveaxis(t, 2, 0) for t in (u, w, a_qk, q_dec, k_dec, g_last))
    s0 = jnp.zeros((b, h, dk, dv), f32)
    _, o = lax.scan(step, s0, xs)
    return jnp.transpose(o, (1, 0, 3, 2, 4)).reshape(b, s, h, dv)


def deltanet_mixer(h, w_in, conv_w, a_log, dt_bias, norm_w, w_out):
    b, s, _ = h.shape
    proj = h @ w_in
    qkv, z, bt, at = jnp.split(proj, [DN_QKV, DN_QKV + DN_VAL, DN_QKV + DN_VAL + DN_HEADS], axis=-1)
    qkv = jax.nn.silu(causal_depthwise_conv(qkv, conv_w))
    q, k, v = jnp.split(qkv, [DN_KEY, 2 * DN_KEY], axis=-1)
    q = l2_normalize(q.reshape(b, s, DN_HEADS, DN_HEAD_DIM))
    k = l2_normalize(k.reshape(b, s, DN_HEADS, DN_HEAD_DIM))
    v = v.reshape(b, s, DN_HEADS, DN_HEAD_DIM)
    beta = jax.nn.sigmoid(bt.astype(jnp.float32))
    g = -jnp.exp(a_log.astype(jnp.float32)) * jax.nn.softplus(at.astype(jnp.float32) + dt_bias)
    o = gated_delta_rule(q, k, v, g, beta)
    o = o * lax.rsqrt(jnp.mean(o * o, axis=-1, keepdims=True) + RMS_EPS) * norm_w
    o = o * jax.nn.silu(z.reshape(b, s, DN_HEADS, DN_HEAD_DIM).astype(jnp.float32))
    return o.reshape(b, s, DN_VAL).astype(h.dtype) @ w_out


def conformer_conv_mixer(h, w_in, dw_w, dw_b, ln_g, ln_b, w_out):
    val, gate = jnp.split(h @ w_in, 2, axis=-1)
    u = val * jax.nn.sigmoid(gate)
    u = causal_depthwise_conv(u, dw_w) + dw_b
    u = jax.nn.silu(layer_norm(u, ln_g, ln_b))
    return u @ w_out


def sq_relu_mlp(h, w1, w2):
    return jnp.square(jax.nn.relu(h @ w1)) @ w2


def setup_inputs(seed: int = 0) -> dict:
    key = jax.random.key(seed)
    ks = jax.random.split(key, 24)
    nrm = jax.random.normal
    f32 = jnp.float32
    x = nrm(ks[0], (BATCH, SEQ, D_MODEL), f32)
    c = nrm(ks[1], (BATCH, D_MODEL), f32)
    ada_w = nrm(ks[2], (DEPTH, D_MODEL, N_MOD * D_MODEL), f32) * (0.1 * D_MODEL ** -0.5)
    ada_b = nrm(ks[3], (DEPTH, N_MOD * D_MODEL), f32) * 0.01
    ln_g = 1.0 + 0.01 * nrm(ks[4], (DEPTH, 2, D_MODEL), f32)
    ln_b = 0.01 * nrm(ks[5], (DEPTH, 2, D_MODEL), f32)
    dn_w_in = nrm(ks[6], (N_A_LAYERS, D_MODEL, DN_IN), f32) * D_MODEL ** -0.5
    dn_conv_w = nrm(ks[7], (N_A_LAYERS, DN_CONV, DN_QKV), f32) * DN_CONV ** -0.5
    dn_a_log = jnp.log(jax.random.uniform(ks[8], (N_A_LAYERS, DN_HEADS), f32, 1.0, 16.0))
    dt = jnp.exp(jax.random.uniform(ks[9], (N_A_LAYERS, DN_HEADS), f32, float(np.log(1e-3)), float(np.log(1e-1))))
    dn_dt_bias = dt + jnp.log(-jnp.expm1(-dt))
    dn_norm_w = 1.0 + 0.01 * nrm(ks[10], (N_A_LAYERS, DN_HEAD_DIM), f32)
    dn_w_out = nrm(ks[11], (N_A_LAYERS, DN_VAL, D_MODEL), f32) * (BETA_INIT * DN_VAL ** -0.5)
    cf_w_in = nrm(ks[12], (N_B_LAYERS, D_MODEL, 2 * CF_CH), f32) * D_MODEL ** -0.5
    cf_dw_w = nrm(ks[13], (N_B_LAYERS, CF_KERNEL, CF_CH), f32) * CF_KERNEL ** -0.5
    cf_dw_b = 0.01 * nrm(ks[14], (N_B_LAYERS, CF_CH), f32)
    cf_ln_g = 1.0 + 0.01 * nrm(ks[15], (N_B_LAYERS, CF_CH), f32)
    cf_ln_b = 0.01 * nrm(ks[16], (N_B_LAYERS, CF_CH), f32)
    cf_w_out = nrm(ks[17], (N_B_LAYERS, CF_CH, D_MODEL), f32) * (BETA_INIT * CF_CH ** -0.5)
    ff_w1 = nrm(ks[18], (DEPTH, D_MODEL, FF_DIM), f32) * D_MODEL ** -0.5
    ff_w2 = nrm(ks[19], (DEPTH, FF_DIM, D_MODEL), f32) * (BETA_INIT * FF_DIM ** -0.5)
    return {"x": x, "c": c, "ada_w": ada_w, "ada_b": ada_b, "ln_g": ln_g, "ln_b": ln_b,
            "dn_w_in": dn_w_in, "dn_conv_w": dn_conv_w, "dn_a_log": dn_a_log,
            "dn_dt_bias": dn_dt_bias, "dn_norm_w": dn_norm_w, "dn_w_out": dn_w_out,
            "cf_w_in": cf_w_in, "cf_dw_w": cf_dw_w, "cf_dw_b": cf_dw_b, "cf_ln_g": cf_ln_g,
            "cf_ln_b": cf_ln_b, "cf_w_out": cf_w_out, "ff_w1": ff_w1, "ff_w2": ff_w2}


def reference(x, c, ada_w, ada_b, ln_g, ln_b, dn_w_in, dn_conv_w, dn_a_log, dn_dt_bias,
              dn_norm_w, dn_w_out, cf_w_in, cf_dw_w, cf_dw_b, cf_ln_g, cf_ln_b, cf_w_out,
              ff_w1, ff_w2):
    cond = jax.nn.silu(c)
    for i in range(DEPTH):
        mod = cond @ ada_w[i] + ada_b[i]
        sh1, sc1, gt1, sh2, sc2, gt2 = [m[:, None, :] for m in jnp.split(mod, N_MOD, axis=-1)]
        h = x * (1.0 + sc1) + sh1
        j = i // N_MIXERS
        if i % N_MIXERS == 0:
            y = deltanet_mixer(h, dn_w_in[j], dn_conv_w[j], dn_a_log[j], dn_dt_bias[j],
                               dn_norm_w[j], dn_w_out[j])
        else:
            y = conformer_conv_mixer(h, cf_w_in[j], cf_dw_w[j], cf_dw_b[j], cf_ln_g[j],
                                     cf_ln_b[j], cf_w_out[j])
        x = layer_norm(ALPHA * x + (1.0 + gt1) * y, ln_g[i, 0], ln_b[i, 0])
        h = x * (1.0 + sc2) + sh2
        x = layer_norm(ALPHA * x + (1.0 + gt2) * sq_relu_mlp(h, ff_w1[i], ff_w2[i]), ln_g[i, 1], ln_b[i, 1])
    return x
```

```python
import contextlib
import numpy as np
import concourse.bass as bass
import concourse.mybir as mybir
from concourse.bass_utils import run_bass_kernel_spmd

F32 = mybir.dt.float32
BF16 = mybir.dt.bfloat16
AF = mybir.ActivationFunctionType
ALU = mybir.AluOpType
PAGE = 256
ENGS = ("pe", "act", "dve", "pool", "sp")

D = 1024
S = 2048
NB = 4
TB = 512
DEPTH = 4
ALPHA = (2.0 * DEPTH) ** 0.25
LN_EPS = 1e-5
NSLOT = 3
NCORES = 8


def _dsize(dt):
    if dt in (F32, mybir.dt.float32r, mybir.dt.int32, mybir.dt.uint32):
        return 4
    if dt in (BF16, mybir.dt.float16, mybir.dt.int16, mybir.dt.uint16):
        return 2
    raise ValueError(dt)


class Op:
    __slots__ = ("eng", "fn", "deps", "idx", "waits", "signal", "semval", "clock",
                 "is_dma", "key", "keycnt", "gi")


class Prog:
    def __init__(self, nc):
        self.nc = nc
        self.ops = {e: [] for e in ENGS}
        self.all_ops = []
        self.pages = {}
        self.rowbytes = {}
        self.keycnt = {}
        self.final_waits = []

    def _pages(self, ap):
        name = ap.tensor.name
        ds = _dsize(ap.dtype)
        apl = ap.ap
        rb = self.rowbytes[name]
        off = int(ap.offset)
        row_elems = rb // ds
        col = off % row_elems
        ext = 1
        for (st, cnt) in apl[1:]:
            ext += (cnt - 1) * abs(st)
        b0 = col * ds
        b1 = (col + ext) * ds
        assert b1 <= rb, (name, b0, b1, rb, apl, off)
        return [(name, p) for p in range(b0 // PAGE, (b1 - 1) // PAGE + 1)]

    def register(self, name, rowbytes):
        self.rowbytes[name] = rowbytes

    def _mk(self, eng, fn, ins, outs, is_dma=False, key=None):
        o = Op()
        o.eng = eng
        o.fn = fn
        o.is_dma = is_dma
        o.key = key
        o.signal = False
        o.semval = None
        o.waits = []
        o.clock = None
        o.keycnt = 0
        deps = {}
        rpages = []
        wpages = []
        for ap in ins:
            if ap.tensor.name in self.rowbytes:
                rpages += self._pages(ap)
        for ap in outs:
            if ap.tensor.name in self.rowbytes:
                wpages += self._pages(ap)
        for pg in rpages:
            st = self.pages.get(pg)
            if st is not None and st[0] is not None:
                deps[id(st[0])] = (st[0], "raw")
        for pg in wpages:
            st = self.pages.get(pg)
            if st is None:
                continue
            if st[0] is not None and id(st[0]) not in deps:
                deps[id(st[0])] = (st[0], "waw")
            for r in st[1].values():
                if id(r) not in deps:
                    deps[id(r)] = (r, "war")
            for r in st[2]:
                if id(r) not in deps:
                    deps[id(r)] = (r, "war")
        dl = []
        for d, kind in deps.values():
            if (not d.is_dma) and (not is_dma) and d.eng == eng:
                if eng == "pe":
                    continue
            dl.append(d)
        o.deps = dl
        for pg in wpages:
            self.pages[pg] = [o, {}, []]
        for pg in rpages:
            st = self.pages.get(pg)
            if st is None:
                st = [None, {}, []]
                self.pages[pg] = st
            if st[0] is o:
                continue
            if is_dma:
                st[2].append(o)
            else:
                st[1][eng] = o
        o.idx = len(self.ops[eng])
        if is_dma:
            self.keycnt[key] = self.keycnt.get(key, 0) + 1
            o.keycnt = self.keycnt[key]
        self.ops[eng].append(o)
        o.gi = len(self.all_ops)
        self.all_ops.append(o)
        return o

    def op(self, eng, fn, ins, outs):
        return self._mk(eng, fn, ins, outs)

    def dma(self, eng, out, in_, key, final=False):
        o = self._mk(eng, lambda e: e.dma_start(out=out, in_=in_), [in_], [out], is_dma=True, key=key)
        if final:
            self.final_waits.append(o)
        return o

    def resolve(self):
        know = {e: {} for e in ENGS}
        for o in self.all_ops:
            kn = know[o.eng]
            for d in sorted(o.deps, key=lambda d: -d.gi):
                if d.is_dma:
                    ck, cv = ("k", d.key), d.keycnt
                else:
                    ck, cv = d.eng, d.idx
                if kn.get(ck, -1) >= cv:
                    continue
                o.waits.append(d)
                d.signal = True
                for k, v in d.clock.items():
                    if kn.get(k, -1) < v:
                        kn[k] = v
                if kn.get(ck, -1) < cv:
                    kn[ck] = cv
            clk = dict(kn)
            if o.is_dma:
                clk[("k", o.key)] = o.keycnt
            else:
                clk[o.eng] = o.idx
            o.clock = clk
        for o in self.final_waits:
            o.signal = True
        for e in ENGS:
            c = 0
            for o in self.ops[e]:
                if o.is_dma:
                    o.semval = 16 * o.keycnt
                elif o.signal:
                    c += 1
                    o.semval = c

    def emit(self, block, sem_alloc):
        self.resolve()
        esem = {e: sem_alloc("e_" + e) for e in ENGS}
        ksem = {k: sem_alloc("k_" + str(k)) for k in self.keycnt}

        def run(ename, eng):
            for o in self.ops[ename]:
                for d in o.waits:
                    if d.is_dma:
                        eng.wait_ge(ksem[d.key], d.semval)
                    else:
                        eng.wait_ge(esem[d.eng], d.semval)
                ins = o.fn(eng)
                if o.is_dma:
                    ins.then_inc(ksem[o.key], 16)
                elif o.signal:
                    ins.then_inc(esem[ename], 1)
            if ename == "sp":
                for o in self.final_waits:
                    eng.wait_ge(ksem[o.key], o.semval)

        @block.tensor
        def _(e):
            run("pe", e)

        @block.scalar
        def _(e):
            run("act", e)

        @block.vector
        def _(e):
            run("dve", e)

        @block.gpsimd
        def _(e):
            run("pool", e)

        @block.sync
        def _(e):
            run("sp", e)

    def stats(self):
        return {e: (len(self.ops[e]), sum(len(o.waits) for o in self.ops[e]),
                    sum(1 for o in self.ops[e] if o.signal)) for e in ENGS}


PV = {}
_o = 0
for _n, _w in (("cT", 8), ("ada_b", 192), ("ln_g", 64), ("ln_b", 64), ("dn_conv", 192),
               ("dn_norm", 2), ("dn_alog", 16), ("dn_dtb", 16), ("cf_dw", 496), ("cf_dwb", 16),
               ("cf_lng", 16), ("cf_lnb", 16)):
    PV[_n] = _o
    _o += _w
NPV = _o

CST = {"U": 0, "SL": 128, "MS": 256, "ID": 384, "ONES": 512}
NCST = 640

G_ADA = 0
G_LAYER = [48, 48 + 26, 48 + 26 + 22, 48 + 26 + 22 + 26]
NGROUPS = 48 + 26 + 22 + 26 + 22


def _kmajor(w, n0):
    return np.ascontiguousarray(w[:, n0:n0 + 512].reshape(8, 128, 512).transpose(1, 0, 2)).reshape(128, 4096)


def _w2group(w2, o):
    return np.ascontiguousarray(w2[:, o * 128:(o + 1) * 128].reshape(32, 128, 128).transpose(1, 0, 2)).reshape(128, 4096)


def _fm(v):
    v = np.asarray(v, np.float32).reshape(-1, 128)
    return np.ascontiguousarray(v.T)


def prep_shared(inp):
    f = lambda k: np.asarray(inp[k], np.float32)
    wst = np.empty((NGROUPS, 128, 4096), np.float32)
    ada_w = f("ada_w")
    for l in range(4):
        for j in range(12):
            wst[G_ADA + l * 12 + j] = _kmajor(ada_w[l], j * 512)
    ff1, ff2 = f("ff_w1"), f("ff_w2")
    dn_in, dn_out = f("dn_w_in"), f("dn_w_out")
    cf_in, cf_out = f("cf_w_in"), f("cf_w_out")
    for l in range(4):
        j = l // 2
        g = G_LAYER[l]
        if l % 2 == 0:
            for i in range(8):
                wst[g + i] = _kmajor(dn_in[j], i * 512)
            for i in range(2):
                wst[g + 8 + i] = _kmajor(dn_out[j], i * 512)
            g += 10
        else:
            for i in range(4):
                wsel = np.concatenate([cf_in[j][:, i * 256:(i + 1) * 256], cf_in[j][:, 1024 + i * 256:1024 + (i + 1) * 256]], axis=1)
                wst[g + i] = _kmajor(wsel, 0)
            for i in range(2):
                wst[g + 4 + i] = _kmajor(cf_out[j], i * 512)
            g += 6
        for i in range(8):
            wst[g + i] = _kmajor(ff1[l], i * 512)
        for i in range(8):
            wst[g + 8 + i] = _w2group(ff2[l], i)
    wsm = np.ascontiguousarray(dn_in[:, :, 4096:4112].reshape(2, 8, 128, 16).transpose(2, 0, 1, 3)).reshape(128, 256)
    cst = np.zeros((128, NCST), np.float32)
    i = np.arange(128)
    cst[:, CST["U"]:CST["U"] + 128] = (i[:, None] <= i[None, :])
    cst[:, CST["SL"]:CST["SL"] + 128] = (i[None, :] < i[:, None])
    cst[:, CST["MS"]:CST["MS"] + 128] = (i[:, None] < i[None, :])
    cst[:, CST["ID"]:CST["ID"] + 128] = np.eye(128)
    cst[:, CST["ONES"]:CST["ONES"] + 128] = 1.0
    pv = np.zeros((128, NPV), np.float32)
    pv[:, PV["ada_b"]:PV["ada_b"] + 192] = _fm(f("ada_b"))
    pv[:, PV["ln_g"]:PV["ln_g"] + 64] = _fm(f("ln_g"))
    pv[:, PV["ln_b"]:PV["ln_b"] + 64] = _fm(f("ln_b"))
    cw = f("dn_conv_w")
    t = cw.reshape(2, 4, 24, 128).transpose(3, 0, 2, 1)
    pv[:, PV["dn_conv"]:PV["dn_conv"] + 192] = t.reshape(128, 192)
    pv[:, PV["dn_norm"]:PV["dn_norm"] + 2] = f("dn_norm_w").T
    pv[:, PV["dn_alog"]:PV["dn_alog"] + 16] = np.broadcast_to(f("dn_a_log").reshape(1, 16), (128, 16))
    pv[:, PV["dn_dtb"]:PV["dn_dtb"] + 16] = np.broadcast_to(f("dn_dt_bias").reshape(1, 16), (128, 16))
    dw = f("cf_dw_w")
    t = dw.reshape(2, 31, 8, 128).transpose(3, 0, 2, 1)
    pv[:, PV["cf_dw"]:PV["cf_dw"] + 496] = t.reshape(128, 496)
    pv[:, PV["cf_dwb"]:PV["cf_dwb"] + 16] = _fm(f("cf_dw_b"))
    pv[:, PV["cf_lng"]:PV["cf_lng"] + 16] = _fm(f("cf_ln_g"))
    pv[:, PV["cf_lnb"]:PV["cf_lnb"] + 16] = _fm(f("cf_ln_b"))
    return wst, wsm, cst, pv


def prep_masks():
    i = np.arange(128)
    msk = np.zeros((128, 7, 128), np.float32)
    msk[:, 0, :] = (i[:, None] // 8 == i[None, :] // 8)
    msk[:, 5, :] = np.where(i[:, None] <= i[None, :], 0.0, -30000.0)
    msk[:, 6, :] = np.where(i[:, None] < i[None, :], 0.0, -30000.0)
    for n, m in enumerate((8, 16, 32, 64)):
        for a in range(0, 128, 2 * m):
            msk[a:a + m, n + 1, a + m:a + 2 * m] = -1.0
    return msk.reshape(128, 896)


def prep_core(inp, pv_shared, b):
    x = np.asarray(inp["x"], np.float32)[b]
    xin = np.ascontiguousarray(x.T.reshape(8, 128, 2048).transpose(1, 0, 2))
    pv = pv_shared.copy()
    pv[:, PV["cT"]:PV["cT"] + 8] = _fm(np.asarray(inp["c"], np.float32)[b])
    return xin, pv


DBG = {}


def build(plan, dbg_spec=None):
    nc = bass.Bass("TRN2", target_bir_lowering=False)
    xin = nc.dram_tensor("xin", [128, 8, S], F32, kind="ExternalInput").ap()
    pvd = nc.dram_tensor("pv", [128, NPV], F32, kind="ExternalInput").ap()
    cstd = nc.dram_tensor("cst", [128, NCST], F32, kind="ExternalInput").ap()
    wst = nc.dram_tensor("wst", [NGROUPS, 128, 4096], F32, kind="ExternalInput").ap()
    wsmd = nc.dram_tensor("wsm", [128, 256], F32, kind="ExternalInput").ap()
    mskd = nc.dram_tensor("msk", [128, 896], F32, kind="ExternalInput").ap()
    yout = nc.dram_tensor("yout", [128, 8, S], F32, kind="ExternalOutput").ap()

    P = Prog(nc)
    ARENA = 212480
    with contextlib.ExitStack() as es:
        arena = es.enter_context(nc.sbuf_tensor("arena", [128, ARENA // 2], BF16))
        psum = es.enter_context(nc.psum_tensor("psum", [128, 4096], F32))
        P.register("arena", ARENA)
        P.register("psum", 16384)
        cur = [0]

        def sb(shape, dt, at=None):
            n = int(np.prod(shape))
            ds = _dsize(dt)
            if at is None:
                off = cur[0]
                cur[0] += (n * ds + 31) // 32 * 32
            else:
                off = at
            assert off + n * ds <= ARENA, ("arena overflow", off, n * ds)
            a = arena[:, off // 2:(off + n * ds) // 2]
            if dt != BF16:
                a = a.bitcast(dt)
            if len(shape) == 2:
                a = a.rearrange("p (a b) -> p a b", a=shape[0])
            elif len(shape) == 3:
                a = a.rearrange("p (a b c) -> p a b c", a=shape[0], b=shape[1])
            return a

        def bank(i, n=1):
            return psum[:, i * 512:(i + n) * 512]

        psi = [0, 0]
        ring1 = [list(range(8))]
        ring2 = [[4, 6]]

        def ps1():
            r = ring1[0]
            b = r[psi[0] % len(r)]
            psi[0] += 1
            return bank(b)

        def ps2():
            r = ring2[0]
            b = r[psi[1] % len(r)]
            psi[1] += 1
            return bank(b, 2)

        xT = sb([8, S], F32)
        wslot = [sb([4096], BF16) for _ in range(NSLOT)]
        hbuf = sb([8, TB], BF16)
        big = sb([32, TB], BF16)
        mixo = sb([8, TB], BF16)
        lnf = [sb([TB], F32) for _ in range(6)]
        pv = sb([NPV], F32)
        mod = sb([192], F32)
        modp1 = sb([192], F32)
        modga = sb([192], F32)
        condb = sb([8], BF16)
        ones_div_d = sb([128], BF16)
        cstb = sb([NCST], BF16)
        U_b = cstb[:, CST["U"]:CST["U"] + 128]
        SL_b = cstb[:, CST["SL"]:CST["SL"] + 128]
        MS_b = cstb[:, CST["MS"]:CST["MS"] + 128]
        ident_b = cstb[:, CST["ID"]:CST["ID"] + 128]
        ones_b = sb([128], BF16)
        ones128_b = sb([128], BF16)
        ones_div128_b = sb([128], BF16)
        wsmb = sb([2, 8, 16], BF16)
        G0 = cur[0]
        GSZ = ARENA - G0
        sq = big[:, 0:8, :]
        tb16 = big[:, 8:16, :]

        block = es.enter_context(nc.Block())

        P.dma("sp", pv, pvd, "ld_pv")
        P.dma("pool", cstb, cstd, "ld_cst")
        for kc in range(8):
            P.dma("sp", xT[:, kc, :], xin[:, kc, :], "ld_x%d" % kc)
        P.op("dve", lambda e: e.memset(ones_div_d, 1.0 / 1024.0), [], [ones_div_d])
        P.op("dve", lambda e: e.memset(ones_b, 1.0), [], [ones_b])
        P.op("dve", lambda e: e.memset(ones128_b, 128.0), [], [ones128_b])
        P.op("dve", lambda e: e.memset(ones_div128_b, 1.0 / 128.0), [], [ones_div128_b])
        P.dma("pool", wsmb, wsmd.rearrange("p (j k n) -> p j k n", j=2, k=8), "ld_wsm")
        cTv = pv[:, PV["cT"]:PV["cT"] + 8]
        P.op("act", lambda e: e.activation(out=condb, in_=cTv, func=AF.Silu), [cTv], [condb])

        use_ctr = [0]

        def load(g):
            s = use_ctr[0] % NSLOT
            use_ctr[0] += 1
            P.dma("pool", wslot[s], wst[g], "w%d" % s)
            return wslot[s]

        def mm(out, lhsT, rhs, start, stop):
            P.op("pe", lambda e: e.matmul(out=out, lhsT=lhsT, rhs=rhs, start=start, stop=stop),
                 [lhsT, rhs], [out])

        adab = pv[:, PV["ada_b"]:PV["ada_b"] + 192]
        layers_needed = []
        for (l, k) in plan:
            if l not in layers_needed:
                layers_needed.append(l)
        ada_pending = []

        ada_loaded = []

        def ada_load():
            if ada_pending:
                l_, j_ = ada_pending.pop(0)
                slot = load(G_ADA + l_ * 12 + j_).rearrange("p (k n) -> p k n", k=8)
                ada_loaded.append((l_, j_, slot))

        def ada_compute():
            if not ada_loaded:
                return
            l, j, slot = ada_loaded.pop(0)
            pb = ps1()
            for jj in range(4):
                for kc in range(8):
                    mm(pb[:, jj:jj + 1], slot[:, kc, jj * 128:(jj + 1) * 128], condb[:, kc:kc + 1], kc == 0, kc == 7)
            c0 = l * 48 + j * 4
            P.op("dve", lambda e: e.tensor_tensor(out=mod[:, c0:c0 + 4], in0=pb[:, 0:4], in1=adab[:, c0:c0 + 4], op=ALU.add),
                 [pb[:, 0:4], adab[:, c0:c0 + 4]], [mod[:, c0:c0 + 4]])
            if j == 11:
                ada_finish(l)

        def ada_finish(l):
            sl = slice(l * 48, (l + 1) * 48)
            P.op("dve", lambda e: e.tensor_scalar(out=modp1[:, sl], in0=mod[:, sl], scalar1=1.0, scalar2=None, op0=ALU.add),
                 [mod[:, sl]], [modp1[:, sl]])
            P.op("dve", lambda e: e.tensor_scalar(out=modga[:, sl], in0=mod[:, sl], scalar1=1.0, scalar2=1.0 / ALPHA,
                                                  op0=ALU.add, op1=ALU.mult),
                 [mod[:, sl]], [modga[:, sl]])

        def ada_flush():
            while ada_pending or ada_loaded:
                ada_load()
                ada_compute()

        ada_pending.extend((layers_needed[0], j) for j in range(12))
        ada_flush()

        def mcol(l, m, kc):
            c = l * 48 + m * 8 + kc
            return slice(c, c + 1)

        def modulate(l, w, tb):
            for kc in range(8):
                src = xT[:, kc, tb * TB:(tb + 1) * TB]
                dst = hbuf[:, kc, :]
                sc = modp1[:, mcol(l, 1 + 3 * w, kc)]
                sh = mod[:, mcol(l, 0 + 3 * w, kc)]
                if kc % 2 == 0:
                    P.op("act", lambda e, src=src, dst=dst, sc=sc, sh=sh: e.activation(out=dst, in_=src, func=AF.Identity, scale=sc, bias=sh),
                         [src, sc, sh], [dst])
                else:
                    P.op("dve", lambda e, src=src, dst=dst, sc=sc, sh=sh: e.tensor_scalar(out=dst, in0=src, scalar1=sc, scalar2=sh, op0=ALU.mult, op1=ALU.add),
                         [src, sc, sh], [dst])
            return hbuf

        def residual(l, w, tb, o, yps):
            xs = xT[:, o, tb * TB:(tb + 1) * TB]
            ga = modga[:, mcol(l, 2 + 3 * w, o)]
            P.op("dve", lambda e: e.scalar_tensor_tensor(out=xs, in0=yps, scalar=ga, in1=xs, op0=ALU.mult, op1=ALU.add),
                 [yps, ga, xs], [xs])

        def ln_stats(eps, sq=sq, tb16=tb16):
            m2, var, rstd, mr = lnf[0], lnf[1], lnf[2], lnf[3]
            mean_ps = ps1()
            ex2_ps = ps1()
            for o in range(8):
                mm(mean_ps, ones_div_d, tb16[:, o, :], o == 0, o == 7)
            for o in range(8):
                mm(ex2_ps, ones_div_d, sq[:, o, :], o == 0, o == 7)
            P.op("act", lambda e: e.activation(out=m2, in_=mean_ps, func=AF.Square), [mean_ps], [m2])
            P.op("dve", lambda e: e.scalar_tensor_tensor(out=var, in0=ex2_ps, scalar=eps, in1=m2, op0=ALU.add, op1=ALU.subtract),
                 [ex2_ps, m2], [var])
            P.op("act", lambda e: e.activation(out=m2, in_=var, func=AF.Ln), [var], [m2])
            P.op("act", lambda e: e.activation(out=rstd, in_=m2, func=AF.Exp, scale=-0.5), [m2], [rstd])
            P.op("dve", lambda e: e.tensor_tensor(out=mr, in0=mean_ps, in1=rstd, op=ALU.mult), [mean_ps, rstd], [mr])

        def ln_apply(o, src, apply):
            rstd, mr, n1, n2 = lnf[2], lnf[3], lnf[4], lnf[5]
            P.op("dve", lambda e: e.tensor_tensor(out=n1, in0=src, in1=rstd, op=ALU.mult), [src, rstd], [n1])
            P.op("dve", lambda e: e.tensor_tensor(out=n2, in0=n1, in1=mr, op=ALU.subtract), [n1, mr], [n2])
            apply(o, n2)

        def ln_core(srcs, eps, apply, sq=sq, tb16=tb16):
            ln_stats(eps, sq, tb16)
            for o in range(8):
                ln_apply(o, srcs[o], apply)

        def res_ln_parts(l, w, tb, sq=sq, tb16=tb16):
            srcs = [xT[:, o, tb * TB:(tb + 1) * TB] for o in range(8)]

            def partA():
                for o in range(8):
                    s_ = srcs[o]
                    P.op("act", lambda e, s_=s_, o=o: e.activation(out=sq[:, o, :], in_=s_, func=AF.Square), [s_], [sq[:, o, :]])
                    P.op("dve", lambda e, s_=s_, o=o: e.tensor_copy(out=tb16[:, o, :], in_=s_), [s_], [tb16[:, o, :]])
                ln_stats(LN_EPS / (ALPHA * ALPHA), sq, tb16)

            def apply(o, n2):
                c = (l * 2 + w) * 8 + o
                g = pv[:, PV["ln_g"] + c:PV["ln_g"] + c + 1]
                b = pv[:, PV["ln_b"] + c:PV["ln_b"] + c + 1]
                dst = srcs[o]
                P.op("act", lambda e: e.activation(out=dst, in_=n2, func=AF.Identity, scale=g, bias=b), [n2, g, b], [dst])

            return partA, [(lambda o=o: ln_apply(o, srcs[o], apply)) for o in range(8)]

        def res_ln(l, w, tb, sq=sq, tb16=tb16):
            srcs = [xT[:, o, tb * TB:(tb + 1) * TB] for o in range(8)]
            for o in range(8):
                s_ = srcs[o]
                P.op("act", lambda e, s_=s_, o=o: e.activation(out=sq[:, o, :], in_=s_, func=AF.Square), [s_], [sq[:, o, :]])
                P.op("dve", lambda e, s_=s_, o=o: e.tensor_copy(out=tb16[:, o, :], in_=s_), [s_], [tb16[:, o, :]])

            def apply(o, n2):
                c = (l * 2 + w) * 8 + o
                g = pv[:, PV["ln_g"] + c:PV["ln_g"] + c + 1]
                b = pv[:, PV["ln_b"] + c:PV["ln_b"] + c + 1]
                dst = srcs[o]
                P.op("act", lambda e: e.activation(out=dst, in_=n2, func=AF.Identity, scale=g, bias=b), [n2, g, b], [dst])

            ln_core(srcs, LN_EPS / (ALPHA * ALPHA), apply, sq, tb16)

        def mlp_layer(l):
            gbase = G_LAYER[l] + (10 if l % 2 == 0 else 6)
            sq2 = sb([8, TB], BF16, at=G0)
            tb2 = sb([8, TB], BF16, at=G0 + 8 * TB * 2)
            pend = None
            for tb in range(NB):
                h = modulate(l, 1, tb)
                for g in range(8):
                    slot = load(gbase + g).rearrange("p (k n) -> p k n", k=8)
                    for j in range(4):
                        f = g * 4 + j
                        pst = ps1()
                        for kc in range(8):
                            mm(pst, slot[:, kc, j * 128:(j + 1) * 128], h[:, kc, :], kc == 0, kc == 7)
                        a = big[:, f, :]
                        P.op("act", lambda e, a=a, pst=pst: e.activation(out=a, in_=pst, func=AF.Relu), [pst], [a])
                        P.op("dve", lambda e, a=a: e.tensor_tensor(out=a, in0=a, in1=a, op=ALU.mult), [a], [a])
                    if g == 3 and pend is not None:
                        res_ln(l, 1, pend, sq2, tb2)
                        pend = None
                for o in range(8):
                    slot = load(gbase + 8 + o).rearrange("p (f n) -> p f n", f=32)
                    pst = ps1()
                    for f in range(32):
                        mm(pst, slot[:, f, :], big[:, f, :], f == 0, f == 31)
                    residual(l, 1, tb, o, pst)
                pend = tb
            res_ln(l, 1, pend)

        def cf_layer(l):
            j = l // 2
            off = G0
            ubuf = sb([8, 30 + TB], BF16, at=off); off += 8 * (30 + TB) * 2
            off = (off + 31) // 32 * 32
            cv = sb([8, TB], F32, at=off); off += 8 * TB * 4
            diag = []
            for i in range(2):
                diag.append(sb([31, 128], BF16, at=off)); off += 31 * 128 * 2
            sgt = [sb([TB], F32, at=off), sb([TB], F32, at=off + TB * 4)]
            off += 2 * TB * 4
            assert off <= ARENA
            gb = G_LAYER[l]
            P.op("dve", lambda e: e.memset(ubuf[:, :, 0:30], 0.0), [], [ubuf[:, :, 0:30]])
            dcount = [0]

            def phaseB(tb):
                h = hbuf
                for gi in range(4):
                    sw = load(gb + gi).rearrange("p (k n) -> p k n", k=8)
                    for jj in range(2):
                        ch = gi * 2 + jj
                        vps = ps1()
                        gps = ps1()
                        for kc in range(8):
                            mm(vps, sw[:, kc, jj * 128:(jj + 1) * 128], h[:, kc, :], kc == 0, kc == 7)
                        for kc in range(8):
                            mm(gps, sw[:, kc, 256 + jj * 128:256 + (jj + 1) * 128], h[:, kc, :], kc == 0, kc == 7)
                        st = sgt[ch % 2]
                        P.op("act", lambda e, st=st, gps=gps: e.activation(out=st, in_=gps, func=AF.Sigmoid), [gps], [st])
                        if tb > 0:
                            P.op("dve", lambda e, ch=ch: e.tensor_copy(out=ubuf[:, ch, 0:30], in_=ubuf[:, ch, TB:TB + 30]),
                                 [ubuf[:, ch, TB:TB + 30]], [ubuf[:, ch, 0:30]])
                        ud = ubuf[:, ch, 30:30 + TB]
                        P.op("dve", lambda e, ud=ud, vps=vps, st=st: e.tensor_tensor(out=ud, in0=vps, in1=st, op=ALU.mult), [vps, st], [ud])

            def build_diag(ch):
                dg = diag[dcount[0] % 2]
                dcount[0] += 1
                wv = pv[:, PV["cf_dw"] + (j * 8 + ch) * 31:PV["cf_dw"] + (j * 8 + ch + 1) * 31]
                P.op("dve", lambda e: e.tensor_tensor(
                    out=dg, in0=ident_b.unsqueeze(1).to_broadcast([128, 31, 128]),
                    in1=wv.unsqueeze(2).to_broadcast([128, 31, 128]), op=ALU.mult), [ident_b, wv], [dg])
                return dg

            def cf_apply(o, n2):
                g = pv[:, PV["cf_lng"] + j * 8 + o:PV["cf_lng"] + j * 8 + o + 1]
                b = pv[:, PV["cf_lnb"] + j * 8 + o:PV["cf_lnb"] + j * 8 + o + 1]
                dst = mixo[:, o, :]
                P.op("act", lambda e: e.activation(out=dst, in_=n2, func=AF.Silu, scale=g, bias=b), [n2, g, b], [dst])

            modulate(l, 0, 0)
            phaseB(0)
            if NB > 1:
                modulate(l, 0, 1)
            for tb in range(NB):
                dg = build_diag(0)
                lnB = []
                if tb > 0:
                    pa, lnB = res_ln_parts(l, 0, tb - 1)
                    pa()
                for _ in range(3):
                    ada_load()
                for ch in range(8):
                    cps = ps1()
                    for t in range(31):
                        mm(cps, dg[:, t, :], ubuf[:, ch, t:t + TB], t == 0, t == 30)
                    if ch + 1 < 8:
                        dg = build_diag(ch + 1)
                    bcol = pv[:, PV["cf_dwb"] + j * 8 + ch:PV["cf_dwb"] + j * 8 + ch + 1]
                    cvc = cv[:, ch, :]
                    P.op("act", lambda e, cvc=cvc, cps=cps, bcol=bcol: e.activation(out=cvc, in_=cps, func=AF.Identity, bias=bcol),
                         [cps, bcol], [cvc])
                    P.op("act", lambda e, ch=ch, cps=cps, bcol=bcol: e.activation(out=sq[:, ch, :], in_=cps, func=AF.Square, bias=bcol),
                         [cps, bcol], [sq[:, ch, :]])
                    P.op("dve", lambda e, ch=ch, cvc=cvc: e.tensor_copy(out=tb16[:, ch, :], in_=cvc), [cvc], [tb16[:, ch, :]])
                    if lnB:
                        lnB.pop(0)()
                for _ in range(3):
                    ada_compute()
                ln_stats(LN_EPS)
                for o in range(8):
                    ln_apply(o, cv[:, o, :], cf_apply)
                if tb + 1 < NB:
                    phaseB(tb + 1)
                    if tb + 2 < NB:
                        modulate(l, 0, tb + 2)
                for half in range(2):
                    so = load(gb + 4 + half).rearrange("p (k n) -> p k n", k=8)
                    for jj in range(4):
                        o = half * 4 + jj
                        yps = ps1()
                        for kc in range(8):
                            mm(yps, so[:, kc, jj * 128:(jj + 1) * 128], mixo[:, kc, :], kc == 0, kc == 7)
                        residual(l, 0, tb, o, yps)
            res_ln(l, 0, NB - 1)
            ada_flush()

        def gdn_layer(l):
            j = l // 2
            gb = G_LAYER[l]
            ring1[0] = list(range(8))
            NU = 4
            USZ = 13312
            big_hi = int(big.offset) * 2 + 16 * TB * 2
            ubase = [G0, G0 + USZ, G0 + 2 * USZ, big_hi]
            units = []
            for u in range(NU):
                b0 = ubase[u]
                d = {}
                d["kv"] = sb([1024], BF16, at=b0)
                d["ktok"] = d["kv"][:, 0:512]
                d["vtok"] = d["kv"][:, 512:1024]
                d["r"] = d["ktok"]
                d["kg"] = sb([512], BF16, at=b0 + 2048)
                d["kd"] = sb([512], BF16, at=b0 + 3072)
                d["gu"] = sb([512], BF16, at=b0 + 4096)
                d["eg"] = sb([512], BF16, at=b0 + 5120)
                d["dt"] = sb([512], BF16, at=b0 + 6144)
                d["dti"] = sb([512], BF16, at=b0 + 7168)
                d["dts"] = sb([512], BF16, at=b0 + 8192)
                d["pq"] = [sb([1024], BF16, at=b0 + 9216), sb([1024], BF16, at=b0 + 11264)]
                d["u2"] = d["pq"][0][:, 0:512]
                units.append(d)
            o_ = G0 + 3 * USZ
            vnew = lnf[5].bitcast(BF16)[:, 0:512]
            o2 = lnf[5].bitcast(BF16)[:, 512:1024]
            Sf = sb([8, 128], F32, at=o_); o_ += 4096
            Sb = sb([8, 128], BF16, at=o_); o_ += 2048
            rawb = [sb([528], BF16, at=o_), sb([528], BF16, at=o_ + 1056)]; o_ += 2112
            mskb = sb([7, 128], BF16, at=o_); o_ += 1792
            halo = sb([24, 4], BF16, at=big_hi + USZ)
            d4 = [sb([4, 128], BF16, at=big_hi + USZ + 192), sb([4, 128], BF16, at=big_hi + USZ + 192 + 1024)]
            assert USZ + 192 + 2048 <= 16384
            sc_ = {}
            for nm in ("beta", "negb", "g", "s1", "e2", "gam", "eg", "egl", "ekd", "dd"):
                sc_[nm] = sb([4, 8], F32, at=o_); o_ += 128
            nea = sb([8], F32, at=o_); o_ += 32
            gbf = sb([4, 8], BF16, at=o_); o_ += 64
            assert o_ <= ARENA, ("G overflow", o_ - ARENA)
            tmpf, sdf, rrf, ogf = lnf[0], lnf[1], lnf[2], lnf[3]
            sqn = lnf[4].bitcast(BF16)[:, 0:512]

            DBG.update(Sf=Sf, big=big, mixo=mixo, beta=sc_["beta"], g=sc_["g"], gam=sc_["gam"], egl=sc_["egl"], xT=xT)
            P.op("dve", lambda e: e.memset(Sf, 0.0), [], [Sf])
            P.op("dve", lambda e: e.memset(Sb, 0.0), [], [Sb])
            P.dma("pool", mskb, mskd.rearrange("p (a b) -> p a b", a=7), "ld_msk")
            NEGI_b = mskb[:, 5, :]
            NEGS_b = mskb[:, 6, :]
            BD8_b = mskb[:, 0, :]
            NM_b = {8: mskb[:, 1, :], 16: mskb[:, 2, :], 32: mskb[:, 3, :], 64: mskb[:, 4, :]}
            alog = pv[:, PV["dn_alog"] + j * 8:PV["dn_alog"] + j * 8 + 8]
            dtb = pv[:, PV["dn_dtb"] + j * 8:PV["dn_dtb"] + j * 8 + 8]
            normw = pv[:, PV["dn_norm"] + j:PV["dn_norm"] + j + 1]
            P.op("act", lambda e: e.activation(out=nea, in_=alog, func=AF.Exp), [alog], [nea])
            P.op("dve", lambda e: e.tensor_scalar(out=nea, in0=nea, scalar1=-1.0, scalar2=None, op0=ALU.mult), [nea], [nea])
            Uf_ = U_b
            rawc = [0]
            g_ = sc_["g"]

            def bc3(ap2, n):
                return ap2.unsqueeze(2).to_broadcast([128, ap2.shape[1], n])

            def bcm(ap2, n):
                return ap2.unsqueeze(1).to_broadcast([128, n, ap2.shape[1]])

            def v3(ap):
                return ap.rearrange("p (h c) -> p h c", h=4)

            for tb in range(NB):
                h = modulate(l, 0, tb)
                def scalars(h):
                    ba_ps = ps1()[:, 0:64]
                    for i in range(4):
                        for kc in range(8):
                            mm(ba_ps[:, i * 16:(i + 1) * 16], h[:, kc, i * 128:(i + 1) * 128], wsmb[:, j, kc, :], kc == 0, kc == 7)
                    ba3 = ba_ps.rearrange("p (i n) -> p i n", i=4)
                    btv, atv = ba3[:, :, 0:8], ba3[:, :, 8:16]
                    beta, negb, g_, s1, e2 = sc_["beta"], sc_["negb"], sc_["g"], sc_["s1"], sc_["e2"]
                    P.op("act", lambda e: e.activation(out=beta, in_=btv, func=AF.Exp, scale=-1.0), [ba_ps], [beta])
                    P.op("dve", lambda e: e.tensor_scalar(out=beta, in0=beta, scalar1=1.0, scalar2=None, op0=ALU.add), [beta], [beta])
                    P.op("dve", lambda e: e.reciprocal(out=beta, in_=beta), [beta], [beta])
                    P.op("dve", lambda e: e.tensor_scalar(out=negb, in0=beta, scalar1=-1.0, scalar2=None, op0=ALU.mult), [beta], [negb])
                    P.op("dve", lambda e: e.tensor_tensor(out=s1, in0=atv, in1=dtb.unsqueeze(1).to_broadcast([128, 4, 8]), op=ALU.add), [ba_ps, dtb], [s1])
                    P.op("act", lambda e: e.activation(out=e2, in_=s1, func=AF.Exp), [s1], [e2])
                    P.op("act", lambda e: e.activation(out=e2, in_=e2, func=AF.Ln, bias=1.0), [e2], [e2])
                    P.op("dve", lambda e: e.tensor_tensor(out=g_, in0=e2, in1=nea.unsqueeze(1).to_broadcast([128, 4, 8]), op=ALU.mult), [e2, nea], [g_])
                    g2 = gbf.rearrange("p i n -> p (i n)")
                    P.op("dve", lambda e: e.tensor_copy(out=gbf, in_=g_), [g_], [gbf])
                    gam_ps = ps1()[:, 0:32]
                    gl_ps = ps1()[:, 0:32]
                    mm(gam_ps, U_b, g2, True, True)
                    mm(gl_ps, ones_b, g2, True, True)
                    gam, eg, egl, ekd, dd = sc_["gam"], sc_["eg"], sc_["egl"], sc_["ekd"], sc_["dd"]
                    f2 = lambda a: a.rearrange("p i n -> p (i n)")
                    P.op("act", lambda e: e.activation(out=f2(gam), in_=gam_ps, func=AF.Identity), [gam_ps], [gam])
                    P.op("act", lambda e: e.activation(out=f2(eg), in_=gam_ps, func=AF.Exp), [gam_ps], [eg])
                    P.op("act", lambda e: e.activation(out=f2(egl), in_=gl_ps, func=AF.Exp), [gl_ps], [egl])
                    P.op("dve", lambda e: e.tensor_tensor(out=f2(dd), in0=gl_ps, in1=f2(gam), op=ALU.subtract), [gl_ps, gam], [dd])
                    P.op("act", lambda e: e.activation(out=ekd, in_=dd, func=AF.Exp), [dd], [ekd])
                scalars(h)

                for hg in range(2):
                    HB = 0
                    pendc = []

                    def convB(rb_, dg, dst):
                        cps = ps1()
                        for t in range(4):
                            mm(cps, dg[:, t, :], rb_[:, t:t + TB], t == 0, t == 3)
                        P.op("act", lambda e: e.activation(out=dst, in_=cps, func=AF.Silu), [cps], [dst])

                    for ty in range(3):
                        slot = load(gb + 2 * ty + hg).rearrange("p (k n) -> p k n", k=8)
                        for hh in range(4):
                            ch = ty * 8 + hg * 4 + hh
                            rps = ps1()
                            for kc in range(8):
                                mm(rps, slot[:, kc, hh * 128:(hh + 1) * 128], h[:, kc, :], kc == 0, kc == 7)
                            if pendc:
                                convB(*pendc.pop(0))
                            rb_ = rawb[rawc[0] % 2]
                            dg = d4[rawc[0] % 2]
                            rawc[0] += 1
                            wv = pv[:, PV["dn_conv"] + (j * 24 + ch) * 4:PV["dn_conv"] + (j * 24 + ch) * 4 + 4]
                            P.op("dve", lambda e, dg=dg, wv=wv: e.tensor_tensor(out=dg, in0=bcm(ident_b, 4), in1=bc3(wv, 128), op=ALU.mult),
                                 [ident_b, wv], [dg])
                            P.op("act", lambda e, rb_=rb_, rps=rps: e.activation(out=rb_[:, 3:3 + TB], in_=rps, func=AF.Identity), [rps], [rb_[:, 3:3 + TB]])
                            if tb == 0:
                                P.op("dve", lambda e, rb_=rb_: e.memset(rb_[:, 0:3], 0.0), [], [rb_[:, 0:3]])
                            else:
                                P.op("dve", lambda e, rb_=rb_, ch=ch: e.tensor_copy(out=rb_[:, 0:3], in_=halo[:, ch, 0:3]), [halo[:, ch, 0:3]], [rb_[:, 0:3]])
                            P.op("dve", lambda e, rb_=rb_, ch=ch: e.tensor_copy(out=halo[:, ch, 0:3], in_=rb_[:, TB:TB + 3]), [rb_[:, TB:TB + 3]], [halo[:, ch, 0:3]])
                            pendc.append((rb_, dg, big[:, HB + ty * 4 + hh, :]))
                    slot = load(gb + 6 + hg).rearrange("p (k n) -> p k n", k=8)
                    for hh in range(4):
                        zps = ps1()
                        for kc in range(8):
                            mm(zps, slot[:, kc, hh * 128:(hh + 1) * 128], h[:, kc, :], kc == 0, kc == 7)
                        dst = big[:, HB + 12 + hh, :]
                        P.op("act", lambda e, dst=dst, zps=zps: e.activation(out=dst, in_=zps, func=AF.Silu), [zps], [dst])
                    while pendc:
                        convB(*pendc.pop(0))
                    jobs = [(ty, hh) for ty in range(2) for hh in range(4)]
                    sqn_t = [lnf[4].bitcast(BF16)[:, 0:512], lnf[4].bitcast(BF16)[:, 512:1024],
                             lnf[5].bitcast(BF16)[:, 0:512], lnf[5].bitcast(BF16)[:, 512:1024]]
                    pend = []

                    def l2a(n, ty, hh):
                        src = big[:, HB + ty * 4 + hh, :]
                        sq_ = sqn_t[n % 4]
                        sd_ = lnf[n % 4]
                        P.op("dve", lambda e: e.tensor_tensor(out=sq_, in0=src, in1=src, op=ALU.mult), [src], [sq_])
                        sps = ps1()
                        mm(sps, ones128_b if ty == 0 else ones_b, sq_, True, True)
                        epsv = 128e-6 if ty == 0 else 1e-6
                        P.op("act", lambda e: e.activation(out=sd_, in_=sps, func=AF.Ln, bias=epsv), [sps], [sd_])
                        return (src, sd_)

                    def l2b(src, sd_):
                        P.op("act", lambda e: e.activation(out=sd_, in_=sd_, func=AF.Exp, scale=-0.5), [sd_], [sd_])
                        P.op("dve", lambda e: e.tensor_tensor(out=src, in0=src, in1=sd_, op=ALU.mult), [src, sd_], [src])

                    for n, (ty, hh) in enumerate(jobs):
                        pend.append(l2a(n, ty, hh))
                        if len(pend) > 2:
                            l2b(*pend.pop(0))
                    while pend:
                        l2b(*pend.pop(0))

                    for _ in range(2):
                        ada_load()
                    ring1[0] = [0, 1, 2, 3]
                    for pair in range(0, 4, NU):
                        tiles = list(range(pair, pair + NU))

                        def cols(i):
                            return slice(i * 128, (i + 1) * 128)

                        def each(fn):
                            for ui, i in enumerate(tiles):
                                fn(units[ui], i)

                        def st1(d, i):
                            psb = ps1().bitcast(BF16)
                            for hh in range(4):
                                kvw = big[:, HB + 4 + hh, cols(i)]
                                P.op("pe", lambda e, kvw=kvw, hh=hh, psb=psb: e.transpose(out=psb[:, hh * 128:(hh + 1) * 128], in_=kvw, identity=ident_b),
                                     [kvw, ident_b], [psb[:, hh * 128:(hh + 1) * 128]])
                            for hh in range(4):
                                vvw = big[:, HB + 8 + hh, cols(i)]
                                P.op("pe", lambda e, vvw=vvw, hh=hh, psb=psb: e.transpose(out=psb[:, 512 + hh * 128:512 + (hh + 1) * 128], in_=vvw, identity=ident_b),
                                     [vvw, ident_b], [psb[:, 512 + hh * 128:512 + (hh + 1) * 128]])
                            P.op("act", lambda e: e.activation(out=d["kv"], in_=psb, func=AF.Identity), [psb], [d["kv"]])
                            gu3 = v3(d["gu"])
                            gsl = gbf[:, i, 4 * hg:4 * hg + 4]
                            P.op("dve", lambda e: e.tensor_tensor(out=gu3, in0=bcm(Uf_, 4), in1=bc3(gsl, 128), op=ALU.mult), [Uf_, gsl], [d["gu"]])
                        each(st1)

                        def st2(d, i):
                            gr = ps1()
                            mm(gr, ones_b, d["gu"], True, True)
                            P.op("act", lambda e: e.activation(out=d["eg"], in_=gr, func=AF.Exp), [gr], [d["eg"]])
                            dfi = ps1()
                            dfs = ps1()
                            for hh in range(4):
                                sl = slice(hh * 128, (hh + 1) * 128)
                                mm(dfi[:, sl], SL_b, d["gu"][:, sl], True, False)
                                mm(dfi[:, sl], ident_b, NEGI_b, False, True)
                            for hh in range(4):
                                sl = slice(hh * 128, (hh + 1) * 128)
                                mm(dfs[:, sl], SL_b, d["gu"][:, sl], True, False)
                                mm(dfs[:, sl], ident_b, NEGS_b, False, True)
                            P.op("act", lambda e: e.activation(out=d["dti"], in_=dfi, func=AF.Exp), [dfi], [d["dti"]])
                            P.op("act", lambda e: e.activation(out=d["dts"], in_=dfs, func=AF.Exp), [dfs], [d["dts"]])
                            egs = sc_["eg"][:, i, 4 * hg:4 * hg + 4]
                            eks = sc_["ekd"][:, i, 4 * hg:4 * hg + 4]
                            P.op("dve", lambda e: e.tensor_tensor(out=v3(d["kg"]), in0=v3(d["ktok"]), in1=bc3(egs, 128), op=ALU.mult), [d["ktok"], egs], [d["kg"]])
                            P.op("dve", lambda e: e.tensor_tensor(out=v3(d["kd"]), in0=v3(d["ktok"]), in1=bc3(eks, 128), op=ALU.mult), [d["ktok"], eks], [d["kd"]])
                        each(st2)

                        def st3(d, i):
                            bsl = sc_["beta"][:, i, 4 * hg:4 * hg + 4]
                            P.op("dve", lambda e: e.tensor_tensor(out=v3(d["dts"]), in0=v3(d["dts"]), in1=bc3(bsl, 128), op=ALU.mult), [d["dts"], bsl], [d["dts"]])
                            kkp = ps1()
                            qkp = ps1()
                            for hh in range(4):
                                kT = big[:, HB + 4 + hh, cols(i)]
                                mm(kkp[:, hh * 128:(hh + 1) * 128], kT, kT, True, True)
                            for hh in range(4):
                                kT = big[:, HB + 4 + hh, cols(i)]
                                qT = big[:, HB + hh, cols(i)]
                                mm(qkp[:, hh * 128:(hh + 1) * 128], kT, qT, True, True)
                            P0 = d["pq"][0][:, 0:512]
                            P.op("dve", lambda e: e.tensor_tensor(out=P0, in0=kkp, in1=d["dts"], op=ALU.mult), [kkp, d["dts"]], [P0])
                            P.op("dve", lambda e: e.tensor_tensor(out=d["dti"], in0=qkp, in1=d["dti"], op=ALU.mult), [qkp, d["dti"]], [d["dti"]])
                            qv = big[:, HB:HB + 4, cols(i)]
                            P.op("dve", lambda e: e.tensor_tensor(out=v3(d["gu"]), in0=qv, in1=v3(d["eg"]), op=ALU.mult), [qv, d["eg"]], [d["gu"]])
                        each(st3)

                        def st4(d, i):
                            psb = ps1().bitcast(BF16)
                            Bp = d["pq"][0][:, 0:512]
                            Ap = d["pq"][0][:, 512:1024]
                            for hh in range(4):
                                src = Bp[:, hh * 128:(hh + 1) * 128]
                                P.op("pe", lambda e, src=src, hh=hh: e.transpose(out=psb[:, hh * 128:(hh + 1) * 128], in_=src, identity=ident_b),
                                     [src, ident_b], [psb[:, hh * 128:(hh + 1) * 128]])
                            P.op("act", lambda e: e.activation(out=Ap, in_=psb[:, 0:512], func=AF.Identity), [psb[:, 0:512]], [Ap])
                            B0, Q0b = d["dts"], d["dt"]
                            P.op("dve", lambda e: e.tensor_tensor(out=v3(B0), in0=v3(Bp), in1=bcm(BD8_b, 4), op=ALU.mult), [Bp, BD8_b], [B0])
                            P.op("dve", lambda e: e.tensor_tensor(out=v3(Q0b), in0=v3(Ap), in1=bcm(BD8_b, 4), op=ALU.mult), [Ap, BD8_b], [Q0b])
                            P.op("dve", lambda e: e.tensor_tensor(out=v3(d["r"]), in0=bcm(ident_b, 4), in1=v3(B0), op=ALU.subtract), [ident_b, B0], [d["r"]])
                        each(st4)

                        def hs(ap, hh):
                            return ap[:, hh * 128:(hh + 1) * 128]

                        def radd(d, lhs):
                            rp = ps1()
                            for hh in range(4):
                                mm(hs(rp, hh), hs(lhs, hh), hs(d["r"], hh), True, True)
                            P.op("dve", lambda e: e.tensor_tensor(out=d["r"], in0=rp, in1=d["r"], op=ALU.add), [rp, d["r"]], [d["r"]])

                        def b1(d, i):
                            B0, Q0b = d["dts"], d["dt"]
                            pq = ps2()
                            for hh in range(4):
                                mm(pq[:, 512 + hh * 128:512 + (hh + 1) * 128], hs(B0, hh), hs(Q0b, hh), True, True)
                            for hh in range(4):
                                mm(pq[:, hh * 128:(hh + 1) * 128], hs(Q0b, hh), hs(B0, hh), True, True)
                            P.op("act", lambda e: e.activation(out=d["pq"][1], in_=pq, func=AF.Identity), [pq], [d["pq"][1]])
                        each(b1)
                        ring1[0] = list(range(8))
                        each(lambda d, i: radd(d, d["pq"][1][:, 512:1024]))

                        def b2(d, i):
                            P1, Q1, Q2 = d["pq"][1][:, 0:512], d["pq"][1][:, 512:1024], d["eg"]
                            qp = ps1()
                            for hh in range(4):
                                mm(hs(qp, hh), hs(P1, hh), hs(Q1, hh), True, True)
                            P.op("act", lambda e: e.activation(out=Q2, in_=qp, func=AF.Identity), [qp], [Q2])
                        each(b2)
                        each(lambda d, i: radd(d, d["eg"]))

                        for m_ in (8, 16, 32, 64):
                            def mg(d, i, m_=m_):
                                Ap = d["pq"][0][:, 512:1024]
                                X, Tm, tmpb = d["pq"][1][:, 0:512], d["pq"][1][:, 512:1024], d["dts"]
                                psb = ps1().bitcast(BF16)
                                for hh in range(4):
                                    src = hs(d["r"], hh)
                                    P.op("pe", lambda e, src=src, hh=hh: e.transpose(out=psb[:, hh * 128:(hh + 1) * 128], in_=src, identity=ident_b),
                                         [src, ident_b], [psb[:, hh * 128:(hh + 1) * 128]])
                                P.op("act", lambda e: e.activation(out=Tm, in_=psb[:, 0:512], func=AF.Identity), [psb[:, 0:512]], [Tm])
                                xp = ps1()
                                for hh in range(4):
                                    mm(hs(xp, hh), hs(Ap, hh), hs(d["r"], hh), True, True)
                                P.op("act", lambda e: e.activation(out=X, in_=xp, func=AF.Identity), [xp], [X])
                            each(mg)

                            def mgb(d, i, m_=m_):
                                X, Tm, tmpb = d["pq"][1][:, 0:512], d["pq"][1][:, 512:1024], d["dts"]
                                yp = ps1()
                                for hh in range(4):
                                    mm(hs(yp, hh), hs(Tm, hh), hs(X, hh), True, True)
                                nm = NM_b[m_]
                                P.op("dve", lambda e: e.tensor_tensor(out=v3(tmpb), in0=v3(yp), in1=bcm(nm, 4), op=ALU.mult), [yp, nm], [tmpb])
                                P.op("dve", lambda e: e.tensor_tensor(out=d["r"], in0=d["r"], in1=tmpb, op=ALU.add), [d["r"], tmpb], [d["r"]])
                            each(mgb)

                        def st5(d, i):
                            ups = ps1()
                            wps = ps1()
                            for hh in range(4):
                                Rh = d["r"][:, hh * 128:(hh + 1) * 128]
                                mm(ups[:, hh * 128:(hh + 1) * 128], Rh, d["vtok"][:, hh * 128:(hh + 1) * 128], True, True)
                            for hh in range(4):
                                Rh = d["r"][:, hh * 128:(hh + 1) * 128]
                                mm(wps[:, hh * 128:(hh + 1) * 128], d["kg"][:, hh * 128:(hh + 1) * 128], Rh, True, True)
                            bsl = sc_["beta"][:, i, 4 * hg:4 * hg + 4]
                            P.op("dve", lambda e: e.tensor_tensor(out=v3(d["u2"]), in0=v3(ups), in1=bc3(bsl, 128), op=ALU.mult), [ups, bsl], [d["u2"]])
                            wTb = d["kg"]
                            d["wT"] = wTb
                            P.op("act", lambda e: e.activation(out=wTb, in_=wps, func=AF.Identity), [wps], [wTb])
                        each(st5)

                        ring1[0] = list(range(8))
                        Sbh = Sb[:, 4 * hg:4 * hg + 4, :]
                        Sfh = Sf[:, 4 * hg:4 * hg + 4, :]
                        gate_pend = []

                        def gate(d, i, ops_):
                            P.op("act", lambda e: e.activation(out=o2, in_=ops_, func=AF.Square), [ops_], [o2])
                            rps2 = ps1()
                            mm(rps2, ones_div128_b, o2, True, True)
                            P.op("act", lambda e: e.activation(out=sdf, in_=rps2, func=AF.Ln, bias=1e-6), [rps2], [sdf])
                            P.op("act", lambda e: e.activation(out=rrf, in_=sdf, func=AF.Exp, scale=-0.5), [sdf], [rrf])
                            P.op("dve", lambda e: e.tensor_tensor(out=ogf, in0=ops_, in1=rrf, op=ALU.mult), [ops_, rrf], [ogf])
                            zv = big[:, HB + 12:HB + 16, cols(i)]
                            dsto = mixo[:, 4 * hg:4 * hg + 4, cols(i)]
                            P.op("dve", lambda e: e.scalar_tensor_tensor(out=dsto, in0=v3(ogf), scalar=normw, in1=zv, op0=ALU.mult, op1=ALU.mult),
                                 [ogf, normw, zv], [dsto])

                        def state(d, i):
                            Sbh = Sb[:, 4 * hg:4 * hg + 4, :]
                            Sfh = Sf[:, 4 * hg:4 * hg + 4, :]
                            nbs = sc_["negb"][:, i, 4 * hg:4 * hg + 4]
                            egls = sc_["egl"][:, i, 4 * hg:4 * hg + 4]
                            P.op("dve", lambda e: e.tensor_tensor(out=Sfh, in0=Sfh, in1=bc3(egls, 128), op=ALU.mult), [Sfh, egls], [Sfh])
                            wsp = ps1()
                            for hh in range(4):
                                mm(wsp[:, hh * 128:(hh + 1) * 128], d["wT"][:, hh * 128:(hh + 1) * 128], Sbh[:, hh, :], True, True)
                            P.op("dve", lambda e: e.tensor_tensor(out=v3(tmpf), in0=v3(wsp), in1=bc3(nbs, 128), op=ALU.mult), [wsp, nbs], [tmpf])
                            P.op("dve", lambda e: e.tensor_tensor(out=vnew, in0=tmpf, in1=d["u2"], op=ALU.add), [tmpf, d["u2"]], [vnew])
                            ops_ = ps1()
                            for hh in range(4):
                                sl = slice(hh * 128, (hh + 1) * 128)
                                mm(ops_[:, sl], Sbh[:, hh, :], d["gu"][:, sl], True, False)
                                mm(ops_[:, sl], vnew[:, sl], d["dti"][:, sl], False, True)
                            dsp = ps1()
                            for hh in range(4):
                                sl = slice(hh * 128, (hh + 1) * 128)
                                mm(dsp[:, sl], d["kd"][:, sl], vnew[:, sl], True, True)
                            P.op("dve", lambda e: e.tensor_tensor(out=Sbh, in0=Sfh, in1=v3(dsp), op=ALU.add), [Sfh, dsp], [Sbh])
                            P.op("dve", lambda e: e.tensor_tensor(out=Sfh, in0=Sfh, in1=v3(dsp), op=ALU.add), [Sfh, dsp], [Sfh])
                            return ops_

                        for ui, i in enumerate(tiles):
                            ops_ = state(units[ui], i)
                            if gate_pend:
                                gate(*gate_pend.pop(0))
                            gate_pend.append((units[ui], i, ops_))
                        while gate_pend:
                            gate(*gate_pend.pop(0))
                        ring1[0] = list(range(8))
                    for _ in range(2):
                        ada_compute()

                for half in range(2):
                    so = load(gb + 8 + half).rearrange("p (k n) -> p k n", k=8)
                    for jj in range(4):
                        o = half * 4 + jj
                        yps = ps1()
                        for kc in range(8):
                            mm(yps, so[:, kc, jj * 128:(jj + 1) * 128], mixo[:, kc, :], kc == 0, kc == 7)
                        residual(l, 0, tb, o, yps)
                res_ln(l, 0, tb)
            ring1[0] = list(range(8))
            ada_flush()

        for (l, kind) in plan:
            if kind == "mlp":
                mlp_layer(l)
            else:
                nxt = layers_needed.index(l) + 1
                if nxt < len(layers_needed):
                    ada_pending.extend((layers_needed[nxt], j) for j in range(12))
                if l % 2 == 0:
                    gdn_layer(l)
                else:
                    cf_layer(l)

        for tb in range(NB):
            P.dma("sp", yout[:, :, tb * TB:(tb + 1) * TB], xT[:, :, tb * TB:(tb + 1) * TB], "st%d" % tb, final=True)

        def sem_alloc(name):
            return es.enter_context(nc.semaphore(name))

        P.emit(block, sem_alloc)
        print("[build] ops/waits/signals:", P.stats(), "arena used", cur[0], "G", GSZ, flush=True)
    return nc


FULL_PLAN = [(l, k) for l in range(DEPTH) for k in ("mix", "mlp")]


def run_plan(inputs, plan, trace=False):
    wst, wsm, cst, pvs = prep_shared(inputs)
    msk = prep_masks()
    in_maps = []
    for b in range(NCORES):
        xin, pv = prep_core(inputs, pvs, b)
        in_maps.append({"xin": xin, "pv": pv, "cst": cst, "wst": wst, "wsm": wsm, "msk": msk})
    nc = build(plan)
    res = run_bass_kernel_spmd(nc, in_maps, core_ids=list(range(NCORES)), trace=trace)
    out = np.empty((NCORES, S, D), np.float32)
    for b in range(NCORES):
        y = res.results[b]["yout"]
        out[b] = y.transpose(1, 0, 2).reshape(D, S).T
    return out, res


def kernel(**inputs):
    out, _ = run_plan(inputs, FULL_PLAN)
    return out
```

```python
import contextlib
import numpy as np
import concourse.bass as bass
import concourse.mybir as mybir
from concourse.bass_utils import run_bass_kernel_spmd

F32 = mybir.dt.float32
BF16 = mybir.dt.bfloat16
AF = mybir.ActivationFunctionType
ALU = mybir.AluOpType
PAGE = 256
ENGS = ("pe", "act", "dve", "pool", "sp")

D = 1024
S = 2048
NB = 4
TB = 512
DEPTH = 4
ALPHA = (2.0 * DEPTH) ** 0.25
LN_EPS = 1e-5
NSLOT = 3
NCORES = 8


def _dsize(dt):
    if dt in (F32, mybir.dt.float32r, mybir.dt.int32, mybir.dt.uint32):
        return 4
    if dt in (BF16, mybir.dt.float16, mybir.dt.int16, mybir.dt.uint16):
        return 2
    raise ValueError(dt)


class Op:
    __slots__ = ("eng", "fn", "deps", "idx", "waits", "signal", "semval", "clock",
                 "is_dma", "key", "keycnt", "gi")


class Prog:
    def __init__(self, nc):
        self.nc = nc
        self.ops = {e: [] for e in ENGS}
        self.all_ops = []
        self.pages = {}
        self.rowbytes = {}
        self.keycnt = {}
        self.final_waits = []

    def _pages(self, ap):
        name = ap.tensor.name
        ds = _dsize(ap.dtype)
        apl = ap.ap
        rb = self.rowbytes[name]
        off = int(ap.offset)
        row_elems = rb // ds
        col = off % row_elems
        ext = 1
        for (st, cnt) in apl[1:]:
            ext += (cnt - 1) * abs(st)
        b0 = col * ds
        b1 = (col + ext) * ds
        assert b1 <= rb, (name, b0, b1, rb, apl, off)
        return [(name, p) for p in range(b0 // PAGE, (b1 - 1) // PAGE + 1)]

    def register(self, name, rowbytes):
        self.rowbytes[name] = rowbytes

    def _mk(self, eng, fn, ins, outs, is_dma=False, key=None):
        o = Op()
        o.eng = eng
        o.fn = fn
        o.is_dma = is_dma
        o.key = key
        o.signal = False
        o.semval = None
        o.waits = []
        o.clock = None
        o.keycnt = 0
        deps = {}
        rpages = []
        wpages = []
        for ap in ins:
            if ap.tensor.name in self.rowbytes:
                rpages += self._pages(ap)
        for ap in outs:
            if ap.tensor.name in self.rowbytes:
                wpages += self._pages(ap)
        for pg in rpages:
            st = self.pages.get(pg)
            if st is not None and st[0] is not None:
                deps[id(st[0])] = (st[0], "raw")
        for pg in wpages:
            st = self.pages.get(pg)
            if st is None:
                continue
            if st[0] is not None and id(st[0]) not in deps:
                deps[id(st[0])] = (st[0], "waw")
            for r in st[1].values():
                if id(r) not in deps:
                    deps[id(r)] = (r, "war")
            for r in st[2]:
                if id(r) not in deps:
                    deps[id(r)] = (r, "war")
        dl = []
        for d, kind in deps.values():
            if (not d.is_dma) and (not is_dma) and d.eng == eng:
                if eng == "pe":
                    continue
            dl.append(d)
        o.deps = dl
        for pg in wpages:
            self.pages[pg] = [o, {}, []]
        for pg in rpages:
            st = self.pages.get(pg)
            if st is None:
                st = [None, {}, []]
                self.pages[pg] = st
            if st[0] is o:
                continue
            if is_dma:
                st[2].append(o)
            else:
                st[1][eng] = o
        o.idx = len(self.ops[eng])
        if is_dma:
            self.keycnt[key] = self.keycnt.get(key, 0) + 1
            o.keycnt = self.keycnt[key]
        self.ops[eng].append(o)
        o.gi = len(self.all_ops)
        self.all_ops.append(o)
        return o

    def op(self, eng, fn, ins, outs):
        return self._mk(eng, fn, ins, outs)

    def dma(self, eng, out, in_, key, final=False):
        o = self._mk(eng, lambda e: e.dma_start(out=out, in_=in_), [in_], [out], is_dma=True, key=key)
        if final:
            self.final_waits.append(o)
        return o

    def resolve(self):
        know = {e: {} for e in ENGS}
        for o in self.all_ops:
            kn = know[o.eng]
            for d in sorted(o.deps, key=lambda d: -d.gi):
                if d.is_dma:
                    ck, cv = ("k", d.key), d.keycnt
                else:
                    ck, cv = d.eng, d.idx
                if kn.get(ck, -1) >= cv:
                    continue
                o.waits.append(d)
                d.signal = True
                for k, v in d.clock.items():
                    if kn.get(k, -1) < v:
                        kn[k] = v
                if kn.get(ck, -1) < cv:
                    kn[ck] = cv
            clk = dict(kn)
            if o.is_dma:
                clk[("k", o.key)] = o.keycnt
            else:
                clk[o.eng] = o.idx
            o.clock = clk
        for o in self.final_waits:
            o.signal = True
        for e in ENGS:
            c = 0
            for o in self.ops[e]:
                if o.is_dma:
                    o.semval = 16 * o.keycnt
                elif o.signal:
                    c += 1
                    o.semval = c

    def emit(self, block, sem_alloc):
        self.resolve()
        esem = {e: sem_alloc("e_" + e) for e in ENGS}
        ksem = {k: sem_alloc("k_" + str(k)) for k in self.keycnt}

        def run(ename, eng):
            for o in self.ops[ename]:
                for d in o.waits:
                    if d.is_dma:
                        eng.wait_ge(ksem[d.key], d.semval)
                    else:
                        eng.wait_ge(esem[d.eng], d.semval)
                ins = o.fn(eng)
                if o.is_dma:
                    ins.then_inc(ksem[o.key], 16)
                elif o.signal:
                    ins.then_inc(esem[ename], 1)
            if ename == "sp":
                for o in self.final_waits:
                    eng.wait_ge(ksem[o.key], o.semval)

        @block.tensor
        def _(e):
            run("pe", e)

        @block.scalar
        def _(e):
            run("act", e)

        @block.vector
        def _(e):
            run("dve", e)

        @block.gpsimd
        def _(e):
            run("pool", e)

        @block.sync
        def _(e):
            run("sp", e)

    def stats(self):
        return {e: (len(self.ops[e]), sum(len(o.waits) for o in self.ops[e]),
                    sum(1 for o in self.ops[e] if o.signal)) for e in ENGS}


PV = {}
_o = 0
for _n, _w in (("cT", 8), ("ada_b", 192), ("ln_g", 64), ("ln_b", 64), ("dn_conv", 192),
               ("dn_norm", 2), ("dn_alog", 16), ("dn_dtb", 16), ("cf_dw", 496), ("cf_dwb", 16),
               ("cf_lng", 16), ("cf_lnb", 16)):
    PV[_n] = _o
    _o += _w
NPV = _o

CST = {"U": 0, "SL": 128, "MS": 256, "ID": 384, "ONES": 512}
NCST = 640

G_ADA = 0
G_LAYER = [48, 48 + 26, 48 + 26 + 22, 48 + 26 + 22 + 26]
NGROUPS = 48 + 26 + 22 + 26 + 22


def _kmajor(w, n0):
    return np.ascontiguousarray(w[:, n0:n0 + 512].reshape(8, 128, 512).transpose(1, 0, 2)).reshape(128, 4096)


def _w2group(w2, o):
    return np.ascontiguousarray(w2[:, o * 128:(o + 1) * 128].reshape(32, 128, 128).transpose(1, 0, 2)).reshape(128, 4096)


def _fm(v):
    v = np.asarray(v, np.float32).reshape(-1, 128)
    return np.ascontiguousarray(v.T)


def prep_shared(inp):
    f = lambda k: np.asarray(inp[k], np.float32)
    wst = np.empty((NGROUPS, 128, 4096), np.float32)
    ada_w = f("ada_w")
    for l in range(4):
        for j in range(12):
            wst[G_ADA + l * 12 + j] = _kmajor(ada_w[l], j * 512)
    ff1, ff2 = f("ff_w1"), f("ff_w2")
    dn_in, dn_out = f("dn_w_in"), f("dn_w_out")
    cf_in, cf_out = f("cf_w_in"), f("cf_w_out")
    for l in range(4):
        j = l // 2
        g = G_LAYER[l]
        if l % 2 == 0:
            for i in range(8):
                wst[g + i] = _kmajor(dn_in[j], i * 512)
            for i in range(2):
                wst[g + 8 + i] = _kmajor(dn_out[j], i * 512)
            g += 10
        else:
            for i in range(4):
                wsel = np.concatenate([cf_in[j][:, i * 256:(i + 1) * 256], cf_in[j][:, 1024 + i * 256:1024 + (i + 1) * 256]], axis=1)
                wst[g + i] = _kmajor(wsel, 0)
            for i in range(2):
                wst[g + 4 + i] = _kmajor(cf_out[j], i * 512)
            g += 6
        for i in range(8):
            wst[g + i] = _kmajor(ff1[l], i * 512)
        for i in range(8):
            wst[g + 8 + i] = _w2group(ff2[l], i)
    wsm = np.ascontiguousarray(dn_in[:, :, 4096:4112].reshape(2, 8, 128, 16).transpose(2, 0, 1, 3)).reshape(128, 256)
    cst = np.zeros((128, NCST), np.float32)
    i = np.arange(128)
    cst[:, CST["U"]:CST["U"] + 128] = (i[:, None] <= i[None, :])
    cst[:, CST["SL"]:CST["SL"] + 128] = (i[None, :] < i[:, None])
    cst[:, CST["MS"]:CST["MS"] + 128] = (i[:, None] < i[None, :])
    cst[:, CST["ID"]:CST["ID"] + 128] = np.eye(128)
    cst[:, CST["ONES"]:CST["ONES"] + 128] = 1.0
    pv = np.zeros((128, NPV), np.float32)
    pv[:, PV["ada_b"]:PV["ada_b"] + 192] = _fm(f("ada_b"))
    pv[:, PV["ln_g"]:PV["ln_g"] + 64] = _fm(f("ln_g"))
    pv[:, PV["ln_b"]:PV["ln_b"] + 64] = _fm(f("ln_b"))
    cw = f("dn_conv_w")
    t = cw.reshape(2, 4, 24, 128).transpose(3, 0, 2, 1)
    pv[:, PV["dn_conv"]:PV["dn_conv"] + 192] = t.reshape(128, 192)
    pv[:, PV["dn_norm"]:PV["dn_norm"] + 2] = f("dn_norm_w").T
    pv[:, PV["dn_alog"]:PV["dn_alog"] + 16] = np.broadcast_to(f("dn_a_log").reshape(1, 16), (128, 16))
    pv[:, PV["dn_dtb"]:PV["dn_dtb"] + 16] = np.broadcast_to(f("dn_dt_bias").reshape(1, 16), (128, 16))
    dw = f("cf_dw_w")
    t = dw.reshape(2, 31, 8, 128).transpose(3, 0, 2, 1)
    pv[:, PV["cf_dw"]:PV["cf_dw"] + 496] = t.reshape(128, 496)
    pv[:, PV["cf_dwb"]:PV["cf_dwb"] + 16] = _fm(f("cf_dw_b"))
    pv[:, PV["cf_lng"]:PV["cf_lng"] + 16] = _fm(f("cf_ln_g"))
    pv[:, PV["cf_lnb"]:PV["cf_lnb"] + 16] = _fm(f("cf_ln_b"))
    return wst, wsm, cst, pv


def prep_masks():
    i = np.arange(128)
    msk = np.zeros((128, 7, 128), np.float32)
    msk[:, 0, :] = (i[:, None] // 8 == i[None, :] // 8)
    msk[:, 5, :] = np.where(i[:, None] <= i[None, :], 0.0, -30000.0)
    msk[:, 6, :] = np.where(i[:, None] < i[None, :], 0.0, -30000.0)
    for n, m in enumerate((8, 16, 32, 64)):
        for a in range(0, 128, 2 * m):
            msk[a:a + m, n + 1, a + m:a + 2 * m] = -1.0
    return msk.reshape(128, 896)


def prep_core(inp, pv_shared, b):
    x = np.asarray(inp["x"], np.float32)[b]
    xin = np.ascontiguousarray(x.T.reshape(8, 128, 2048).transpose(1, 0, 2))
    pv = pv_shared.copy()
    pv[:, PV["cT"]:PV["cT"] + 8] = _fm(np.asarray(inp["c"], np.float32)[b])
    return xin, pv


DBG = {}


def build(plan, dbg_spec=None):
    nc = bass.Bass("TRN2", target_bir_lowering=False)
    xin = nc.dram_tensor("xin", [128, 8, S], F32, kind="ExternalInput").ap()
    pvd = nc.dram_tensor("pv", [128, NPV], F32, kind="ExternalInput").ap()
    cstd = nc.dram_tensor("cst", [128, NCST], F32, kind="ExternalInput").ap()
    wst = nc.dram_tensor("wst", [NGROUPS, 128, 4096], F32, kind="ExternalInput").ap()
    wsmd = nc.dram_tensor("wsm", [128, 256], F32, kind="ExternalInput").ap()
    mskd = nc.dram_tensor("msk", [128, 896], F32, kind="ExternalInput").ap()
    yout = nc.dram_tensor("yout", [128, 8, S], F32, kind="ExternalOutput").ap()

    P = Prog(nc)
    ARENA = 212480
    with contextlib.ExitStack() as es:
        arena = es.enter_context(nc.sbuf_tensor("arena", [128, ARENA // 2], BF16))
        psum = es.enter_context(nc.psum_tensor("psum", [128, 4096], F32))
        P.register("arena", ARENA)
        P.register("psum", 16384)
        cur = [0]

        def sb(shape, dt, at=None):
            n = int(np.prod(shape))
            ds = _dsize(dt)
            if at is None:
                off = cur[0]
                cur[0] += (n * ds + 31) // 32 * 32
            else:
                off = at
            assert off + n * ds <= ARENA, ("arena overflow", off, n * ds)
            a = arena[:, off // 2:(off + n * ds) // 2]
            if dt != BF16:
                a = a.bitcast(dt)
            if len(shape) == 2:
                a = a.rearrange("p (a b) -> p a b", a=shape[0])
            elif len(shape) == 3:
                a = a.rearrange("p (a b c) -> p a b c", a=shape[0], b=shape[1])
            return a

        def bank(i, n=1):
            return psum[:, i * 512:(i + n) * 512]

        psi = [0, 0]
        ring1 = [list(range(8))]
        ring2 = [[2, 4, 6]]

        def ps1():
            r = ring1[0]
            b = r[psi[0] % len(r)]
            psi[0] += 1
            return bank(b)

        def ps2():
            r = ring2[0]
            b = r[psi[1] % len(r)]
            psi[1] += 1
            return bank(b, 2)

        xT = sb([8, S], F32)
        wslot = [sb([4096], BF16) for _ in range(NSLOT)]
        hbuf = sb([8, TB], BF16)
        big = sb([32, TB], BF16)
        mixo = sb([8, TB], BF16)
        lnf = [sb([TB], F32) for _ in range(6)]
        pv = sb([NPV], F32)
        mod = sb([192], F32)
        modp1 = sb([192], F32)
        modga = sb([192], F32)
        condb = sb([8], BF16)
        ones_div_d = sb([128], BF16)
        cstb = sb([NCST], BF16)
        U_b = cstb[:, CST["U"]:CST["U"] + 128]
        SL_b = cstb[:, CST["SL"]:CST["SL"] + 128]
        MS_b = cstb[:, CST["MS"]:CST["MS"] + 128]
        ident_b = cstb[:, CST["ID"]:CST["ID"] + 128]
        ones_b = sb([128], BF16)
        ones128_b = sb([128], BF16)
        ones_div128_b = sb([128], BF16)
        wsmb = sb([2, 8, 16], BF16)
        G0 = cur[0]
        GSZ = ARENA - G0
        sq = big[:, 0:8, :]
        tb16 = big[:, 8:16, :]

        block = es.enter_context(nc.Block())

        P.dma("sp", pv, pvd, "ld_pv")
        P.dma("pool", cstb, cstd, "ld_cst")
        for kc in range(8):
            P.dma("sp", xT[:, kc, :], xin[:, kc, :], "ld_x%d" % kc)
        P.op("dve", lambda e: e.memset(ones_div_d, 1.0 / 1024.0), [], [ones_div_d])
        P.op("dve", lambda e: e.memset(ones_b, 1.0), [], [ones_b])
        P.op("dve", lambda e: e.memset(ones128_b, 128.0), [], [ones128_b])
        P.op("dve", lambda e: e.memset(ones_div128_b, 1.0 / 128.0), [], [ones_div128_b])
        P.dma("pool", wsmb, wsmd.rearrange("p (j k n) -> p j k n", j=2, k=8), "ld_wsm")
        cTv = pv[:, PV["cT"]:PV["cT"] + 8]
        P.op("act", lambda e: e.activation(out=condb, in_=cTv, func=AF.Silu), [cTv], [condb])

        use_ctr = [0]

        def load(g):
            s = use_ctr[0] % NSLOT
            use_ctr[0] += 1
            P.dma("pool", wslot[s], wst[g], "w%d" % s)
            return wslot[s]

        def mm(out, lhsT, rhs, start, stop):
            P.op("pe", lambda e: e.matmul(out=out, lhsT=lhsT, rhs=rhs, start=start, stop=stop),
                 [lhsT, rhs], [out])

        adab = pv[:, PV["ada_b"]:PV["ada_b"] + 192]
        layers_needed = []
        for (l, k) in plan:
            if l not in layers_needed:
                layers_needed.append(l)
        ada_pending = []

        ada_loaded = []

        def ada_load():
            if ada_pending:
                l_, j_ = ada_pending.pop(0)
                slot = load(G_ADA + l_ * 12 + j_).rearrange("p (k n) -> p k n", k=8)
                ada_loaded.append((l_, j_, slot))

        def ada_compute():
            if not ada_loaded:
                return
            l, j, slot = ada_loaded.pop(0)
            pb = ps1()
            for jj in range(4):
                for kc in range(8):
                    mm(pb[:, jj:jj + 1], slot[:, kc, jj * 128:(jj + 1) * 128], condb[:, kc:kc + 1], kc == 0, kc == 7)
            c0 = l * 48 + j * 4
            P.op("dve", lambda e: e.tensor_tensor(out=mod[:, c0:c0 + 4], in0=pb[:, 0:4], in1=adab[:, c0:c0 + 4], op=ALU.add),
                 [pb[:, 0:4], adab[:, c0:c0 + 4]], [mod[:, c0:c0 + 4]])
            if j == 11:
                ada_finish(l)

        def ada_finish(l):
            sl = slice(l * 48, (l + 1) * 48)
            P.op("dve", lambda e: e.tensor_scalar(out=modp1[:, sl], in0=mod[:, sl], scalar1=1.0, scalar2=None, op0=ALU.add),
                 [mod[:, sl]], [modp1[:, sl]])
            P.op("dve", lambda e: e.tensor_scalar(out=modga[:, sl], in0=mod[:, sl], scalar1=1.0, scalar2=1.0 / ALPHA,
                                                  op0=ALU.add, op1=ALU.mult),
                 [mod[:, sl]], [modga[:, sl]])

        def ada_flush():
            while ada_pending or ada_loaded:
                ada_load()
                ada_compute()

        ada_pending.extend((layers_needed[0], j) for j in range(12))
        ada_flush()

        def mcol(l, m, kc):
            c = l * 48 + m * 8 + kc
            return slice(c, c + 1)

        def modulate(l, w, tb):
            for kc in range(8):
                src = xT[:, kc, tb * TB:(tb + 1) * TB]
                dst = hbuf[:, kc, :]
                sc = modp1[:, mcol(l, 1 + 3 * w, kc)]
                sh = mod[:, mcol(l, 0 + 3 * w, kc)]
                if kc % 2 == 0:
                    P.op("act", lambda e, src=src, dst=dst, sc=sc, sh=sh: e.activation(out=dst, in_=src, func=AF.Identity, scale=sc, bias=sh),
                         [src, sc, sh], [dst])
                else:
                    P.op("dve", lambda e, src=src, dst=dst, sc=sc, sh=sh: e.tensor_scalar(out=dst, in0=src, scalar1=sc, scalar2=sh, op0=ALU.mult, op1=ALU.add),
                         [src, sc, sh], [dst])
            return hbuf

        def residual(l, w, tb, o, yps):
            xs = xT[:, o, tb * TB:(tb + 1) * TB]
            ga = modga[:, mcol(l, 2 + 3 * w, o)]
            P.op("dve", lambda e: e.scalar_tensor_tensor(out=xs, in0=yps, scalar=ga, in1=xs, op0=ALU.mult, op1=ALU.add),
                 [yps, ga, xs], [xs])

        def ln_stats(eps, sq=sq, tb16=tb16):
            m2, var, rstd, mr = lnf[0], lnf[1], lnf[2], lnf[3]
            mean_ps = ps1()
            ex2_ps = ps1()
            for o in range(8):
                mm(mean_ps, ones_div_d, tb16[:, o, :], o == 0, o == 7)
            for o in range(8):
                mm(ex2_ps, ones_div_d, sq[:, o, :], o == 0, o == 7)
            P.op("act", lambda e: e.activation(out=m2, in_=mean_ps, func=AF.Square), [mean_ps], [m2])
            P.op("dve", lambda e: e.scalar_tensor_tensor(out=var, in0=ex2_ps, scalar=eps, in1=m2, op0=ALU.add, op1=ALU.subtract),
                 [ex2_ps, m2], [var])
            P.op("act", lambda e: e.activation(out=m2, in_=var, func=AF.Ln), [var], [m2])
            P.op("act", lambda e: e.activation(out=rstd, in_=m2, func=AF.Exp, scale=-0.5), [m2], [rstd])
            P.op("dve", lambda e: e.tensor_tensor(out=mr, in0=mean_ps, in1=rstd, op=ALU.mult), [mean_ps, rstd], [mr])

        def ln_apply(o, src, apply):
            rstd, mr, n1, n2 = lnf[2], lnf[3], lnf[4], lnf[5]
            P.op("dve", lambda e: e.tensor_tensor(out=n1, in0=src, in1=rstd, op=ALU.mult), [src, rstd], [n1])
            P.op("dve", lambda e: e.tensor_tensor(out=n2, in0=n1, in1=mr, op=ALU.subtract), [n1, mr], [n2])
            apply(o, n2)

        def ln_core(srcs, eps, apply, sq=sq, tb16=tb16):
            ln_stats(eps, sq, tb16)
            for o in range(8):
                ln_apply(o, srcs[o], apply)

        def res_ln_parts(l, w, tb, sq=sq, tb16=tb16):
            srcs = [xT[:, o, tb * TB:(tb + 1) * TB] for o in range(8)]

            def partA():
                for o in range(8):
                    s_ = srcs[o]
                    P.op("act", lambda e, s_=s_, o=o: e.activation(out=sq[:, o, :], in_=s_, func=AF.Square), [s_], [sq[:, o, :]])
                    P.op("dve", lambda e, s_=s_, o=o: e.tensor_copy(out=tb16[:, o, :], in_=s_), [s_], [tb16[:, o, :]])
                ln_stats(LN_EPS / (ALPHA * ALPHA), sq, tb16)

            def apply(o, n2):
                c = (l * 2 + w) * 8 + o
                g = pv[:, PV["ln_g"] + c:PV["ln_g"] + c + 1]
                b = pv[:, PV["ln_b"] + c:PV["ln_b"] + c + 1]
                dst = srcs[o]
                P.op("act", lambda e: e.activation(out=dst, in_=n2, func=AF.Identity, scale=g, bias=b), [n2, g, b], [dst])

            return partA, [(lambda o=o: ln_apply(o, srcs[o], apply)) for o in range(8)]

        def res_ln(l, w, tb, sq=sq, tb16=tb16):
            srcs = [xT[:, o, tb * TB:(tb + 1) * TB] for o in range(8)]
            for o in range(8):
                s_ = srcs[o]
                P.op("act", lambda e, s_=s_, o=o: e.activation(out=sq[:, o, :], in_=s_, func=AF.Square), [s_], [sq[:, o, :]])
                P.op("dve", lambda e, s_=s_, o=o: e.tensor_copy(out=tb16[:, o, :], in_=s_), [s_], [tb16[:, o, :]])

            def apply(o, n2):
                c = (l * 2 + w) * 8 + o
                g = pv[:, PV["ln_g"] + c:PV["ln_g"] + c + 1]
                b = pv[:, PV["ln_b"] + c:PV["ln_b"] + c + 1]
                dst = srcs[o]
                P.op("act", lambda e: e.activation(out=dst, in_=n2, func=AF.Identity, scale=g, bias=b), [n2, g, b], [dst])

            ln_core(srcs, LN_EPS / (ALPHA * ALPHA), apply, sq, tb16)

        def mlp_layer(l):
            gbase = G_LAYER[l] + (10 if l % 2 == 0 else 6)
            sq2 = sb([8, TB], BF16, at=G0)
            tb2 = sb([8, TB], BF16, at=G0 + 8 * TB * 2)
            pend = None
            for tb in range(NB):
                h = modulate(l, 1, tb)
                for g in range(8):
                    slot = load(gbase + g).rearrange("p (k n) -> p k n", k=8)
                    for j in range(4):
                        f = g * 4 + j
                        pst = ps1()
                        for kc in range(8):
                            mm(pst, slot[:, kc, j * 128:(j + 1) * 128], h[:, kc, :], kc == 0, kc == 7)
                        a = big[:, f, :]
                        P.op("act", lambda e, a=a, pst=pst: e.activation(out=a, in_=pst, func=AF.Relu), [pst], [a])
                        P.op("dve", lambda e, a=a: e.tensor_tensor(out=a, in0=a, in1=a, op=ALU.mult), [a], [a])
                    if g == 3 and pend is not None:
                        res_ln(l, 1, pend, sq2, tb2)
                        pend = None
                for o in range(8):
                    slot = load(gbase + 8 + o).rearrange("p (f n) -> p f n", f=32)
                    pst = ps1()
                    for f in range(32):
                        mm(pst, slot[:, f, :], big[:, f, :], f == 0, f == 31)
                    residual(l, 1, tb, o, pst)
                pend = tb
            res_ln(l, 1, pend)

        def cf_layer(l):
            j = l // 2
            off = G0
            ubuf = sb([8, 30 + TB], BF16, at=off); off += 8 * (30 + TB) * 2
            off = (off + 31) // 32 * 32
            cv = sb([8, TB], F32, at=off); off += 8 * TB * 4
            diag = []
            for i in range(2):
                diag.append(sb([31, 128], BF16, at=off)); off += 31 * 128 * 2
            sgt = [sb([TB], F32, at=off), sb([TB], F32, at=off + TB * 4)]
            off += 2 * TB * 4
            assert off <= ARENA
            gb = G_LAYER[l]
            P.op("dve", lambda e: e.memset(ubuf[:, :, 0:30], 0.0), [], [ubuf[:, :, 0:30]])
            dcount = [0]

            def phaseB(tb):
                h = hbuf
                for gi in range(4):
                    sw = load(gb + gi).rearrange("p (k n) -> p k n", k=8)
                    for jj in range(2):
                        ch = gi * 2 + jj
                        vps = ps1()
                        gps = ps1()
                        for kc in range(8):
                            mm(vps, sw[:, kc, jj * 128:(jj + 1) * 128], h[:, kc, :], kc == 0, kc == 7)
                        for kc in range(8):
                            mm(gps, sw[:, kc, 256 + jj * 128:256 + (jj + 1) * 128], h[:, kc, :], kc == 0, kc == 7)
                        st = sgt[ch % 2]
                        P.op("act", lambda e, st=st, gps=gps: e.activation(out=st, in_=gps, func=AF.Sigmoid), [gps], [st])
                        if tb > 0:
                            P.op("dve", lambda e, ch=ch: e.tensor_copy(out=ubuf[:, ch, 0:30], in_=ubuf[:, ch, TB:TB + 30]),
                                 [ubuf[:, ch, TB:TB + 30]], [ubuf[:, ch, 0:30]])
                        ud = ubuf[:, ch, 30:30 + TB]
                        P.op("dve", lambda e, ud=ud, vps=vps, st=st: e.tensor_tensor(out=ud, in0=vps, in1=st, op=ALU.mult), [vps, st], [ud])

            def build_diag(ch):
                dg = diag[dcount[0] % 2]
                dcount[0] += 1
                wv = pv[:, PV["cf_dw"] + (j * 8 + ch) * 31:PV["cf_dw"] + (j * 8 + ch + 1) * 31]
                P.op("dve", lambda e: e.tensor_tensor(
                    out=dg, in0=ident_b.unsqueeze(1).to_broadcast([128, 31, 128]),
                    in1=wv.unsqueeze(2).to_broadcast([128, 31, 128]), op=ALU.mult), [ident_b, wv], [dg])
                return dg

            def cf_apply(o, n2):
                g = pv[:, PV["cf_lng"] + j * 8 + o:PV["cf_lng"] + j * 8 + o + 1]
                b = pv[:, PV["cf_lnb"] + j * 8 + o:PV["cf_lnb"] + j * 8 + o + 1]
                dst = mixo[:, o, :]
                P.op("act", lambda e: e.activation(out=dst, in_=n2, func=AF.Silu, scale=g, bias=b), [n2, g, b], [dst])

            modulate(l, 0, 0)
            phaseB(0)
            if NB > 1:
                modulate(l, 0, 1)
            for tb in range(NB):
                dg = build_diag(0)
                lnB = []
                if tb > 0:
                    pa, lnB = res_ln_parts(l, 0, tb - 1)
                    pa()
                for _ in range(3):
                    ada_load()
                for ch in range(8):
                    cps = ps1()
                    for t in range(31):
                        mm(cps, dg[:, t, :], ubuf[:, ch, t:t + TB], t == 0, t == 30)
                    if ch + 1 < 8:
                        dg = build_diag(ch + 1)
                    bcol = pv[:, PV["cf_dwb"] + j * 8 + ch:PV["cf_dwb"] + j * 8 + ch + 1]
                    cvc = cv[:, ch, :]
                    P.op("act", lambda e, cvc=cvc, cps=cps, bcol=bcol: e.activation(out=cvc, in_=cps, func=AF.Identity, bias=bcol),
                         [cps, bcol], [cvc])
                    P.op("act", lambda e, ch=ch, cps=cps, bcol=bcol: e.activation(out=sq[:, ch, :], in_=cps, func=AF.Square, bias=bcol),
                         [cps, bcol], [sq[:, ch, :]])
                    P.op("dve", lambda e, ch=ch, cvc=cvc: e.tensor_copy(out=tb16[:, ch, :], in_=cvc), [cvc], [tb16[:, ch, :]])
                    if lnB:
                        lnB.pop(0)()
                for _ in range(3):
                    ada_compute()
                ln_stats(LN_EPS)
                for o in range(8):
                    ln_apply(o, cv[:, o, :], cf_apply)
                if tb + 1 < NB:
                    phaseB(tb + 1)
                    if tb + 2 < NB:
                        modulate(l, 0, tb + 2)
                for half in range(2):
                    so = load(gb + 4 + half).rearrange("p (k n) -> p k n", k=8)
                    for jj in range(4):
                        o = half * 4 + jj
                        yps = ps1()
                        for kc in range(8):
                            mm(yps, so[:, kc, jj * 128:(jj + 1) * 128], mixo[:, kc, :], kc == 0, kc == 7)
                        residual(l, 0, tb, o, yps)
            res_ln(l, 0, NB - 1)
            ada_flush()

        def gdn_layer(l):
            j = l // 2
            gb = G_LAYER[l]
            ring1[0] = [0, 1, 2, 3]
            NU = 4
            USZ = 13312
            big_hi = int(big.offset) * 2 + 16 * TB * 2
            ubase = [G0, G0 + USZ, G0 + 2 * USZ, big_hi]
            units = []
            for u in range(NU):
                b0 = ubase[u]
                d = {}
                d["kv"] = sb([1024], BF16, at=b0)
                d["ktok"] = d["kv"][:, 0:512]
                d["vtok"] = d["kv"][:, 512:1024]
                d["r"] = d["ktok"]
                d["kg"] = sb([512], BF16, at=b0 + 2048)
                d["kd"] = sb([512], BF16, at=b0 + 3072)
                d["gu"] = sb([512], BF16, at=b0 + 4096)
                d["eg"] = sb([512], BF16, at=b0 + 5120)
                d["dt"] = sb([512], BF16, at=b0 + 6144)
                d["dti"] = sb([512], BF16, at=b0 + 7168)
                d["dts"] = sb([512], BF16, at=b0 + 8192)
                d["pq"] = [sb([1024], BF16, at=b0 + 9216), sb([1024], BF16, at=b0 + 11264)]
                d["u2"] = d["pq"][0][:, 0:512]
                units.append(d)
            o_ = G0 + 3 * USZ
            vnew = lnf[5].bitcast(BF16)[:, 0:512]
            o2 = lnf[5].bitcast(BF16)[:, 512:1024]
            Sf = sb([8, 128], F32, at=o_); o_ += 4096
            Sb = sb([8, 128], BF16, at=o_); o_ += 2048
            rawb = [sb([528], BF16, at=o_), sb([528], BF16, at=o_ + 1056)]; o_ += 2112
            mskb = sb([7, 128], BF16, at=o_); o_ += 1792
            halo = sb([24, 4], BF16, at=big_hi + USZ)
            d4 = [sb([4, 128], BF16, at=big_hi + USZ + 192), sb([4, 128], BF16, at=big_hi + USZ + 192 + 1024)]
            assert USZ + 192 + 2048 <= 16384
            sc_ = {}
            for nm in ("beta", "negb", "g", "s1", "e2", "gam", "eg", "egl", "ekd", "dd"):
                sc_[nm] = sb([4, 8], F32, at=o_); o_ += 128
            nea = sb([8], F32, at=o_); o_ += 32
            gbf = sb([4, 8], BF16, at=o_); o_ += 64
            assert o_ <= ARENA, ("G overflow", o_ - ARENA)
            tmpf, sdf, rrf, ogf = lnf[0], lnf[1], lnf[2], lnf[3]
            sqn = lnf[4].bitcast(BF16)[:, 0:512]

            DBG.update(Sf=Sf, big=big, mixo=mixo, beta=sc_["beta"], g=sc_["g"], gam=sc_["gam"], egl=sc_["egl"], xT=xT)
            P.op("dve", lambda e: e.memset(Sf, 0.0), [], [Sf])
            P.op("dve", lambda e: e.memset(Sb, 0.0), [], [Sb])
            P.dma("pool", mskb, mskd.rearrange("p (a b) -> p a b", a=7), "ld_msk")
            NEGI_b = mskb[:, 5, :]
            NEGS_b = mskb[:, 6, :]
            BD8_b = mskb[:, 0, :]
            NM_b = {8: mskb[:, 1, :], 16: mskb[:, 2, :], 32: mskb[:, 3, :], 64: mskb[:, 4, :]}
            alog = pv[:, PV["dn_alog"] + j * 8:PV["dn_alog"] + j * 8 + 8]
            dtb = pv[:, PV["dn_dtb"] + j * 8:PV["dn_dtb"] + j * 8 + 8]
            normw = pv[:, PV["dn_norm"] + j:PV["dn_norm"] + j + 1]
            P.op("act", lambda e: e.activation(out=nea, in_=alog, func=AF.Exp), [alog], [nea])
            P.op("dve", lambda e: e.tensor_scalar(out=nea, in0=nea, scalar1=-1.0, scalar2=None, op0=ALU.mult), [nea], [nea])
            Uf_ = U_b
            rawc = [0]
            g_ = sc_["g"]

            def bc3(ap2, n):
                return ap2.unsqueeze(2).to_broadcast([128, ap2.shape[1], n])

            def bcm(ap2, n):
                return ap2.unsqueeze(1).to_broadcast([128, n, ap2.shape[1]])

            def v3(ap):
                return ap.rearrange("p (h c) -> p h c", h=4)

            for tb in range(NB):
                h = modulate(l, 0, tb)
                def scalars(h):
                    ba_ps = ps1()[:, 0:64]
                    for i in range(4):
                        for kc in range(8):
                            mm(ba_ps[:, i * 16:(i + 1) * 16], h[:, kc, i * 128:(i + 1) * 128], wsmb[:, j, kc, :], kc == 0, kc == 7)
                    ba3 = ba_ps.rearrange("p (i n) -> p i n", i=4)
                    btv, atv = ba3[:, :, 0:8], ba3[:, :, 8:16]
                    beta, negb, g_, s1, e2 = sc_["beta"], sc_["negb"], sc_["g"], sc_["s1"], sc_["e2"]
                    P.op("act", lambda e: e.activation(out=beta, in_=btv, func=AF.Exp, scale=-1.0), [ba_ps], [beta])
                    P.op("dve", lambda e: e.tensor_scalar(out=beta, in0=beta, scalar1=1.0, scalar2=None, op0=ALU.add), [beta], [beta])
                    P.op("dve", lambda e: e.reciprocal(out=beta, in_=beta), [beta], [beta])
                    P.op("dve", lambda e: e.tensor_scalar(out=negb, in0=beta, scalar1=-1.0, scalar2=None, op0=ALU.mult), [beta], [negb])
                    P.op("dve", lambda e: e.tensor_tensor(out=s1, in0=atv, in1=dtb.unsqueeze(1).to_broadcast([128, 4, 8]), op=ALU.add), [ba_ps, dtb], [s1])
                    P.op("act", lambda e: e.activation(out=e2, in_=s1, func=AF.Exp), [s1], [e2])
                    P.op("act", lambda e: e.activation(out=e2, in_=e2, func=AF.Ln, bias=1.0), [e2], [e2])
                    P.op("dve", lambda e: e.tensor_tensor(out=g_, in0=e2, in1=nea.unsqueeze(1).to_broadcast([128, 4, 8]), op=ALU.mult), [e2, nea], [g_])
                    g2 = gbf.rearrange("p i n -> p (i n)")
                    P.op("dve", lambda e: e.tensor_copy(out=gbf, in_=g_), [g_], [gbf])
                    gam_ps = ps1()[:, 0:32]
                    gl_ps = ps1()[:, 0:32]
                    mm(gam_ps, U_b, g2, True, True)
                    mm(gl_ps, ones_b, g2, True, True)
                    gam, eg, egl, ekd, dd = sc_["gam"], sc_["eg"], sc_["egl"], sc_["ekd"], sc_["dd"]
                    f2 = lambda a: a.rearrange("p i n -> p (i n)")
                    P.op("act", lambda e: e.activation(out=f2(gam), in_=gam_ps, func=AF.Identity), [gam_ps], [gam])
                    P.op("act", lambda e: e.activation(out=f2(eg), in_=gam_ps, func=AF.Exp), [gam_ps], [eg])
                    P.op("act", lambda e: e.activation(out=f2(egl), in_=gl_ps, func=AF.Exp), [gl_ps], [egl])
                    P.op("dve", lambda e: e.tensor_tensor(out=f2(dd), in0=gl_ps, in1=f2(gam), op=ALU.subtract), [gl_ps, gam], [dd])
                    P.op("act", lambda e: e.activation(out=ekd, in_=dd, func=AF.Exp), [dd], [ekd])
                scalars(h)

                for hg in range(2):
                    HB = 0
                    pendc = []

                    def convB(rb_, dg, dst):
                        cps = ps1()
                        for t in range(4):
                            mm(cps, dg[:, t, :], rb_[:, t:t + TB], t == 0, t == 3)
                        P.op("act", lambda e: e.activation(out=dst, in_=cps, func=AF.Silu), [cps], [dst])

                    for ty in range(3):
                        slot = load(gb + 2 * ty + hg).rearrange("p (k n) -> p k n", k=8)
                        for hh in range(4):
                            ch = ty * 8 + hg * 4 + hh
                            rps = ps1()
                            for kc in range(8):
                                mm(rps, slot[:, kc, hh * 128:(hh + 1) * 128], h[:, kc, :], kc == 0, kc == 7)
                            if pendc:
                                convB(*pendc.pop(0))
                            rb_ = rawb[rawc[0] % 2]
                            dg = d4[rawc[0] % 2]
                            rawc[0] += 1
                            wv = pv[:, PV["dn_conv"] + (j * 24 + ch) * 4:PV["dn_conv"] + (j * 24 + ch) * 4 + 4]
                            P.op("dve", lambda e, dg=dg, wv=wv: e.tensor_tensor(out=dg, in0=bcm(ident_b, 4), in1=bc3(wv, 128), op=ALU.mult),
                                 [ident_b, wv], [dg])
                            P.op("act", lambda e, rb_=rb_, rps=rps: e.activation(out=rb_[:, 3:3 + TB], in_=rps, func=AF.Identity), [rps], [rb_[:, 3:3 + TB]])
                            if tb == 0:
                                P.op("dve", lambda e, rb_=rb_: e.memset(rb_[:, 0:3], 0.0), [], [rb_[:, 0:3]])
                            else:
                                P.op("dve", lambda e, rb_=rb_, ch=ch: e.tensor_copy(out=rb_[:, 0:3], in_=halo[:, ch, 0:3]), [halo[:, ch, 0:3]], [rb_[:, 0:3]])
                            P.op("dve", lambda e, rb_=rb_, ch=ch: e.tensor_copy(out=halo[:, ch, 0:3], in_=rb_[:, TB:TB + 3]), [rb_[:, TB:TB + 3]], [halo[:, ch, 0:3]])
                            pendc.append((rb_, dg, big[:, HB + ty * 4 + hh, :]))
                    slot = load(gb + 6 + hg).rearrange("p (k n) -> p k n", k=8)
                    for hh in range(4):
                        zps = ps1()
                        for kc in range(8):
                            mm(zps, slot[:, kc, hh * 128:(hh + 1) * 128], h[:, kc, :], kc == 0, kc == 7)
                        dst = big[:, HB + 12 + hh, :]
                        P.op("act", lambda e, dst=dst, zps=zps: e.activation(out=dst, in_=zps, func=AF.Silu), [zps], [dst])
                    while pendc:
                        convB(*pendc.pop(0))
                    jobs = [(ty, hh) for ty in range(2) for hh in range(4)]
                    sqn_t = [lnf[4].bitcast(BF16)[:, 0:512], lnf[4].bitcast(BF16)[:, 512:1024],
                             lnf[5].bitcast(BF16)[:, 0:512], lnf[5].bitcast(BF16)[:, 512:1024]]
                    pend = []

                    def l2a(n, ty, hh):
                        src = big[:, HB + ty * 4 + hh, :]
                        sq_ = sqn_t[n % 4]
                        sd_ = lnf[n % 4]
                        P.op("dve", lambda e: e.tensor_tensor(out=sq_, in0=src, in1=src, op=ALU.mult), [src], [sq_])
                        sps = ps1()
                        mm(sps, ones128_b if ty == 0 else ones_b, sq_, True, True)
                        epsv = 128e-6 if ty == 0 else 1e-6
                        P.op("act", lambda e: e.activation(out=sd_, in_=sps, func=AF.Ln, bias=epsv), [sps], [sd_])
                        return (src, sd_)

                    def l2b(src, sd_):
                        P.op("act", lambda e: e.activation(out=sd_, in_=sd_, func=AF.Exp, scale=-0.5), [sd_], [sd_])
                        P.op("dve", lambda e: e.tensor_tensor(out=src, in0=src, in1=sd_, op=ALU.mult), [src, sd_], [src])

                    for n, (ty, hh) in enumerate(jobs):
                        pend.append(l2a(n, ty, hh))
                        if len(pend) > 2:
                            l2b(*pend.pop(0))
                    while pend:
                        l2b(*pend.pop(0))

                    for _ in range(2):
                        ada_load()
                    for pair in range(0, 4, NU):
                        tiles = list(range(pair, pair + NU))

                        def cols(i):
                            return slice(i * 128, (i + 1) * 128)

                        def each(fn):
                            for ui, i in enumerate(tiles):
                                fn(units[ui], i)

                        def st1(d, i):
                            psb = ps1().bitcast(BF16)
                            for hh in range(4):
                                kvw = big[:, HB + 4 + hh, cols(i)]
                                P.op("pe", lambda e, kvw=kvw, hh=hh, psb=psb: e.transpose(out=psb[:, hh * 128:(hh + 1) * 128], in_=kvw, identity=ident_b),
                                     [kvw, ident_b], [psb[:, hh * 128:(hh + 1) * 128]])
                            for hh in range(4):
                                vvw = big[:, HB + 8 + hh, cols(i)]
                                P.op("pe", lambda e, vvw=vvw, hh=hh, psb=psb: e.transpose(out=psb[:, 512 + hh * 128:512 + (hh + 1) * 128], in_=vvw, identity=ident_b),
                                     [vvw, ident_b], [psb[:, 512 + hh * 128:512 + (hh + 1) * 128]])
                            P.op("act", lambda e: e.activation(out=d["kv"], in_=psb, func=AF.Identity), [psb], [d["kv"]])
                            gu3 = v3(d["gu"])
                            gsl = gbf[:, i, 4 * hg:4 * hg + 4]
                            P.op("dve", lambda e: e.tensor_tensor(out=gu3, in0=bcm(Uf_, 4), in1=bc3(gsl, 128), op=ALU.mult), [Uf_, gsl], [d["gu"]])
                        each(st1)

                        def st2(d, i):
                            gr = ps1()
                            mm(gr, ones_b, d["gu"], True, True)
                            P.op("act", lambda e: e.activation(out=d["eg"], in_=gr, func=AF.Exp), [gr], [d["eg"]])
                            dfi = ps1()
                            dfs = ps1()
                            for hh in range(4):
                                sl = slice(hh * 128, (hh + 1) * 128)
                                mm(dfi[:, sl], SL_b, d["gu"][:, sl], True, False)
                                mm(dfi[:, sl], ident_b, NEGI_b, False, True)
                            for hh in range(4):
                                sl = slice(hh * 128, (hh + 1) * 128)
                                mm(dfs[:, sl], SL_b, d["gu"][:, sl], True, False)
                                mm(dfs[:, sl], ident_b, NEGS_b, False, True)
                            P.op("act", lambda e: e.activation(out=d["dti"], in_=dfi, func=AF.Exp), [dfi], [d["dti"]])
                            P.op("act", lambda e: e.activation(out=d["dts"], in_=dfs, func=AF.Exp), [dfs], [d["dts"]])
                            egs = sc_["eg"][:, i, 4 * hg:4 * hg + 4]
                            eks = sc_["ekd"][:, i, 4 * hg:4 * hg + 4]
                            P.op("dve", lambda e: e.tensor_tensor(out=v3(d["kg"]), in0=v3(d["ktok"]), in1=bc3(egs, 128), op=ALU.mult), [d["ktok"], egs], [d["kg"]])
                            P.op("dve", lambda e: e.tensor_tensor(out=v3(d["kd"]), in0=v3(d["ktok"]), in1=bc3(eks, 128), op=ALU.mult), [d["ktok"], eks], [d["kd"]])
                        each(st2)

                        def st3(d, i):
                            bsl = sc_["beta"][:, i, 4 * hg:4 * hg + 4]
                            P.op("dve", lambda e: e.tensor_tensor(out=v3(d["dts"]), in0=v3(d["dts"]), in1=bc3(bsl, 128), op=ALU.mult), [d["dts"], bsl], [d["dts"]])
                            kkp = ps1()
                            qkp = ps1()
                            for hh in range(4):
                                kT = big[:, HB + 4 + hh, cols(i)]
                                mm(kkp[:, hh * 128:(hh + 1) * 128], kT, kT, True, True)
                            for hh in range(4):
                                kT = big[:, HB + 4 + hh, cols(i)]
                                qT = big[:, HB + hh, cols(i)]
                                mm(qkp[:, hh * 128:(hh + 1) * 128], kT, qT, True, True)
                            P0 = d["pq"][0][:, 0:512]
                            P.op("dve", lambda e: e.tensor_tensor(out=P0, in0=kkp, in1=d["dts"], op=ALU.mult), [kkp, d["dts"]], [P0])
                            P.op("dve", lambda e: e.tensor_tensor(out=d["dti"], in0=qkp, in1=d["dti"], op=ALU.mult), [qkp, d["dti"]], [d["dti"]])
                            qv = big[:, HB:HB + 4, cols(i)]
                            P.op("dve", lambda e: e.tensor_tensor(out=v3(d["gu"]), in0=qv, in1=v3(d["eg"]), op=ALU.mult), [qv, d["eg"]], [d["gu"]])
                        each(st3)

                        def st4(d, i):
                            psb = ps1().bitcast(BF16)
                            Bp = d["pq"][0][:, 0:512]
                            Ap = d["pq"][0][:, 512:1024]
                            for hh in range(4):
                                src = Bp[:, hh * 128:(hh + 1) * 128]
                                P.op("pe", lambda e, src=src, hh=hh: e.transpose(out=psb[:, hh * 128:(hh + 1) * 128], in_=src, identity=ident_b),
                                     [src, ident_b], [psb[:, hh * 128:(hh + 1) * 128]])
                            P.op("act", lambda e: e.activation(out=Ap, in_=psb[:, 0:512], func=AF.Identity), [psb[:, 0:512]], [Ap])
                            B0, Q0b = d["dts"], d["dt"]
                            P.op("dve", lambda e: e.tensor_tensor(out=v3(B0), in0=v3(Bp), in1=bcm(BD8_b, 4), op=ALU.mult), [Bp, BD8_b], [B0])
                            P.op("dve", lambda e: e.tensor_tensor(out=v3(Q0b), in0=v3(Ap), in1=bcm(BD8_b, 4), op=ALU.mult), [Ap, BD8_b], [Q0b])
                            P.op("dve", lambda e: e.tensor_tensor(out=v3(d["r"]), in0=bcm(ident_b, 4), in1=v3(B0), op=ALU.subtract), [ident_b, B0], [d["r"]])
                        each(st4)

                        def hs(ap, hh):
                            return ap[:, hh * 128:(hh + 1) * 128]

                        def radd(d, lhs):
                            rp = ps1()
                            for hh in range(4):
                                mm(hs(rp, hh), hs(lhs, hh), hs(d["r"], hh), True, True)
                            P.op("dve", lambda e: e.tensor_tensor(out=d["r"], in0=rp, in1=d["r"], op=ALU.add), [rp, d["r"]], [d["r"]])

                        def b1(d, i):
                            B0, Q0b = d["dts"], d["dt"]
                            pq = ps2()
                            for hh in range(4):
                                mm(pq[:, 512 + hh * 128:512 + (hh + 1) * 128], hs(B0, hh), hs(Q0b, hh), True, True)
                            for hh in range(4):
                                mm(pq[:, hh * 128:(hh + 1) * 128], hs(Q0b, hh), hs(B0, hh), True, True)
                            P.op("act", lambda e: e.activation(out=d["pq"][1], in_=pq, func=AF.Identity), [pq], [d["pq"][1]])
                        each(b1)
                        ring1[0] = list(range(8))
                        each(lambda d, i: radd(d, d["pq"][1][:, 512:1024]))

                        def b2(d, i):
                            P1, Q1, Q2 = d["pq"][1][:, 0:512], d["pq"][1][:, 512:1024], d["eg"]
                            qp = ps1()
                            for hh in range(4):
                                mm(hs(qp, hh), hs(P1, hh), hs(Q1, hh), True, True)
                            P.op("act", lambda e: e.activation(out=Q2, in_=qp, func=AF.Identity), [qp], [Q2])
                        each(b2)
                        each(lambda d, i: radd(d, d["eg"]))

                        for m_ in (8, 16, 32, 64):
                            def mg(d, i, m_=m_):
                                Ap = d["pq"][0][:, 512:1024]
                                X, Tm, tmpb = d["pq"][1][:, 0:512], d["pq"][1][:, 512:1024], d["dts"]
                                psb = ps1().bitcast(BF16)
                                for hh in range(4):
                                    src = hs(d["r"], hh)
                                    P.op("pe", lambda e, src=src, hh=hh: e.transpose(out=psb[:, hh * 128:(hh + 1) * 128], in_=src, identity=ident_b),
                                         [src, ident_b], [psb[:, hh * 128:(hh + 1) * 128]])
                                P.op("act", lambda e: e.activation(out=Tm, in_=psb[:, 0:512], func=AF.Identity), [psb[:, 0:512]], [Tm])
                                xp = ps1()
                                for hh in range(4):
                                    mm(hs(xp, hh), hs(Ap, hh), hs(d["r"], hh), True, True)
                                P.op("act", lambda e: e.activation(out=X, in_=xp, func=AF.Identity), [xp], [X])
                            each(mg)

                            def mgb(d, i, m_=m_):
                                X, Tm, tmpb = d["pq"][1][:, 0:512], d["pq"][1][:, 512:1024], d["dts"]
                                yp = ps1()
                                for hh in range(4):
                                    mm(hs(yp, hh), hs(Tm, hh), hs(X, hh), True, True)
                                nm = NM_b[m_]
                                P.op("dve", lambda e: e.tensor_tensor(out=v3(tmpb), in0=v3(yp), in1=bcm(nm, 4), op=ALU.mult), [yp, nm], [tmpb])
                                P.op("dve", lambda e: e.tensor_tensor(out=d["r"], in0=d["r"], in1=tmpb, op=ALU.add), [d["r"], tmpb], [d["r"]])
                            each(mgb)

                        def st5(d, i):
                            ups = ps1()
                            wps = ps1()
                            for hh in range(4):
                                Rh = d["r"][:, hh * 128:(hh + 1) * 128]
                                mm(ups[:, hh * 128:(hh + 1) * 128], Rh, d["vtok"][:, hh * 128:(hh + 1) * 128], True, True)
                            for hh in range(4):
                                Rh = d["r"][:, hh * 128:(hh + 1) * 128]
                                mm(wps[:, hh * 128:(hh + 1) * 128], d["kg"][:, hh * 128:(hh + 1) * 128], Rh, True, True)
                            bsl = sc_["beta"][:, i, 4 * hg:4 * hg + 4]
                            P.op("dve", lambda e: e.tensor_tensor(out=v3(d["u2"]), in0=v3(ups), in1=bc3(bsl, 128), op=ALU.mult), [ups, bsl], [d["u2"]])
                            wTb = d["kg"]
                            d["wT"] = wTb
                            P.op("act", lambda e: e.activation(out=wTb, in_=wps, func=AF.Identity), [wps], [wTb])
                        each(st5)

                        ring1[0] = list(range(8))
                        Sbh = Sb[:, 4 * hg:4 * hg + 4, :]
                        Sfh = Sf[:, 4 * hg:4 * hg + 4, :]
                        gate_pend = []

                        def gate(d, i, ops_):
                            P.op("act", lambda e: e.activation(out=o2, in_=ops_, func=AF.Square), [ops_], [o2])
                            rps2 = ps1()
                            mm(rps2, ones_div128_b, o2, True, True)
                            P.op("act", lambda e: e.activation(out=sdf, in_=rps2, func=AF.Ln, bias=1e-6), [rps2], [sdf])
                            P.op("act", lambda e: e.activation(out=rrf, in_=sdf, func=AF.Exp, scale=-0.5), [sdf], [rrf])
                            P.op("dve", lambda e: e.tensor_tensor(out=ogf, in0=ops_, in1=rrf, op=ALU.mult), [ops_, rrf], [ogf])
                            zv = big[:, HB + 12:HB + 16, cols(i)]
                            dsto = mixo[:, 4 * hg:4 * hg + 4, cols(i)]
                            P.op("dve", lambda e: e.scalar_tensor_tensor(out=dsto, in0=v3(ogf), scalar=normw, in1=zv, op0=ALU.mult, op1=ALU.mult),
                                 [ogf, normw, zv], [dsto])

                        def state(d, i):
                            Sbh = Sb[:, 4 * hg:4 * hg + 4, :]
                            Sfh = Sf[:, 4 * hg:4 * hg + 4, :]
                            nbs = sc_["negb"][:, i, 4 * hg:4 * hg + 4]
                            egls = sc_["egl"][:, i, 4 * hg:4 * hg + 4]
                            P.op("dve", lambda e: e.tensor_tensor(out=Sfh, in0=Sfh, in1=bc3(egls, 128), op=ALU.mult), [Sfh, egls], [Sfh])
                            wsp = ps1()
                            for hh in range(4):
                                mm(wsp[:, hh * 128:(hh + 1) * 128], d["wT"][:, hh * 128:(hh + 1) * 128], Sbh[:, hh, :], True, True)
                            P.op("dve", lambda e: e.tensor_tensor(out=v3(tmpf), in0=v3(wsp), in1=bc3(nbs, 128), op=ALU.mult), [wsp, nbs], [tmpf])
                            P.op("dve", lambda e: e.tensor_tensor(out=vnew, in0=tmpf, in1=d["u2"], op=ALU.add), [tmpf, d["u2"]], [vnew])
                            ops_ = ps1()
                            for hh in range(4):
                                sl = slice(hh * 128, (hh + 1) * 128)
                                mm(ops_[:, sl], Sbh[:, hh, :], d["gu"][:, sl], True, False)
                                mm(ops_[:, sl], vnew[:, sl], d["dti"][:, sl], False, True)
                            dsp = ps1()
                            for hh in range(4):
                                sl = slice(hh * 128, (hh + 1) * 128)
                                mm(dsp[:, sl], d["kd"][:, sl], vnew[:, sl], True, True)
                            P.op("dve", lambda e: e.tensor_tensor(out=Sbh, in0=Sfh, in1=v3(dsp), op=ALU.add), [Sfh, dsp], [Sbh])
                            P.op("dve", lambda e: e.tensor_tensor(out=Sfh, in0=Sfh, in1=v3(dsp), op=ALU.add), [Sfh, dsp], [Sfh])
                            return ops_

                        for ui, i in enumerate(tiles):
                            ops_ = state(units[ui], i)
                            if gate_pend:
                                gate(*gate_pend.pop(0))
                            gate_pend.append((units[ui], i, ops_))
                        while gate_pend:
                            gate(*gate_pend.pop(0))
                        ring1[0] = [0, 1, 2, 3]
                    for _ in range(2):
                        ada_compute()

                for half in range(2):
                    so = load(gb + 8 + half).rearrange("p (k n) -> p k n", k=8)
                    for jj in range(4):
                        o = half * 4 + jj
                        yps = ps1()
                        for kc in range(8):
                            mm(yps, so[:, kc, jj * 128:(jj + 1) * 128], mixo[:, kc, :], kc == 0, kc == 7)
                        residual(l, 0, tb, o, yps)
                res_ln(l, 0, tb)
            ring1[0] = list(range(8))
            ada_flush()

        for (l, kind) in plan:
            if kind == "mlp":
                mlp_layer(l)
            else:
                nxt = layers_needed.index(l) + 1
                if nxt < len(layers_needed):
                    ada_pending.extend((layers_needed[nxt], j) for j in range(12))
                if l % 2 == 0:
                    gdn_layer(l)
                else:
                    cf_layer(l)

        for tb in range(NB):
            P.dma("sp", yout[:, :, tb * TB:(tb + 1) * TB], xT[:, :, tb * TB:(tb + 1) * TB], "st%d" % tb, final=True)

        def sem_alloc(name):
            return es.enter_context(nc.semaphore(name))

        P.emit(block, sem_alloc)
        print("[build] ops/waits/signals:", P.stats(), "arena used", cur[0], "G", GSZ, flush=True)
    return nc


FULL_PLAN = [(l, k) for l in range(DEPTH) for k in ("mix", "mlp")]


def run_plan(inputs, plan, trace=False):
    wst, wsm, cst, pvs = prep_shared(inputs)
    msk = prep_masks()
    in_maps = []
    for b in range(NCORES):
        xin, pv = prep_core(inputs, pvs, b)
        in_maps.append({"xin": xin, "pv": pv, "cst": cst, "wst": wst, "wsm": wsm, "msk": msk})
    nc = build(plan)
    res = run_bass_kernel_spmd(nc, in_maps, core_ids=list(range(NCORES)), trace=trace)
    out = np.empty((NCORES, S, D), np.float32)
    for b in range(NCORES):
        y = res.results[b]["yout"]
        out[b] = y.transpose(1, 0, 2).reshape(D, S).T
    return out, res


def kernel(**inputs):
    out, _ = run_plan(inputs, FULL_PLAN)
    return out
```

```python
import contextlib
import numpy as np
import concourse.bass as bass
import concourse.mybir as mybir
from concourse.bass_utils import run_bass_kernel_spmd

F32 = mybir.dt.float32
BF16 = mybir.dt.bfloat16
AF = mybir.ActivationFunctionType
ALU = mybir.AluOpType
PAGE = 256
ENGS = ("pe", "act", "dve", "pool", "sp")

D = 1024
S = 2048
NB = 4
TB = 512
DEPTH = 4
ALPHA = (2.0 * DEPTH) ** 0.25
LN_EPS = 1e-5
NSLOT = 3
NCORES = 8


def _dsize(dt):
    if dt in (F32, mybir.dt.float32r, mybir.dt.int32, mybir.dt.uint32):
        return 4
    if dt in (BF16, mybir.dt.float16, mybir.dt.int16, mybir.dt.uint16):
        return 2
    raise ValueError(dt)


class Op:
    __slots__ = ("eng", "fn", "deps", "idx", "waits", "signal", "semval", "clock",
                 "is_dma", "key", "keycnt", "gi")


class Prog:
    def __init__(self, nc):
        self.nc = nc
        self.ops = {e: [] for e in ENGS}
        self.all_ops = []
        self.pages = {}
        self.rowbytes = {}
        self.keycnt = {}
        self.final_waits = []

    def _pages(self, ap):
        name = ap.tensor.name
        ds = _dsize(ap.dtype)
        apl = ap.ap
        rb = self.rowbytes[name]
        off = int(ap.offset)
        row_elems = rb // ds
        col = off % row_elems
        ext = 1
        for (st, cnt) in apl[1:]:
            ext += (cnt - 1) * abs(st)
        b0 = col * ds
        b1 = (col + ext) * ds
        assert b1 <= rb, (name, b0, b1, rb, apl, off)
        return [(name, p) for p in range(b0 // PAGE, (b1 - 1) // PAGE + 1)]

    def register(self, name, rowbytes):
        self.rowbytes[name] = rowbytes

    def _mk(self, eng, fn, ins, outs, is_dma=False, key=None):
        o = Op()
        o.eng = eng
        o.fn = fn
        o.is_dma = is_dma
        o.key = key
        o.signal = False
        o.semval = None
        o.waits = []
        o.clock = None
        o.keycnt = 0
        deps = {}
        rpages = []
        wpages = []
        for ap in ins:
            if ap.tensor.name in self.rowbytes:
                rpages += self._pages(ap)
        for ap in outs:
            if ap.tensor.name in self.rowbytes:
                wpages += self._pages(ap)
        for pg in rpages:
            st = self.pages.get(pg)
            if st is not None and st[0] is not None:
                deps[id(st[0])] = (st[0], "raw")
        for pg in wpages:
            st = self.pages.get(pg)
            if st is None:
                continue
            if st[0] is not None and id(st[0]) not in deps:
                deps[id(st[0])] = (st[0], "waw")
            for r in st[1].values():
                if id(r) not in deps:
                    deps[id(r)] = (r, "war")
            for r in st[2]:
                if id(r) not in deps:
                    deps[id(r)] = (r, "war")
        dl = []
        for d, kind in deps.values():
            if (not d.is_dma) and (not is_dma) and d.eng == eng:
                if eng == "pe":
                    continue
            dl.append(d)
        o.deps = dl
        for pg in wpages:
            self.pages[pg] = [o, {}, []]
        for pg in rpages:
            st = self.pages.get(pg)
            if st is None:
                st = [None, {}, []]
                self.pages[pg] = st
            if st[0] is o:
                continue
            if is_dma:
                st[2].append(o)
            else:
                st[1][eng] = o
        o.idx = len(self.ops[eng])
        if is_dma:
            self.keycnt[key] = self.keycnt.get(key, 0) + 1
            o.keycnt = self.keycnt[key]
        self.ops[eng].append(o)
        o.gi = len(self.all_ops)
        self.all_ops.append(o)
        return o

    def op(self, eng, fn, ins, outs):
        return self._mk(eng, fn, ins, outs)

    def dma(self, eng, out, in_, key, final=False):
        o = self._mk(eng, lambda e: e.dma_start(out=out, in_=in_), [in_], [out], is_dma=True, key=key)
        if final:
            self.final_waits.append(o)
        return o

    def resolve(self):
        know = {e: {} for e in ENGS}
        for o in self.all_ops:
            kn = know[o.eng]
            for d in sorted(o.deps, key=lambda d: -d.gi):
                if d.is_dma:
                    ck, cv = ("k", d.key), d.keycnt
                else:
                    ck, cv = d.eng, d.idx
                if kn.get(ck, -1) >= cv:
                    continue
                o.waits.append(d)
                d.signal = True
                for k, v in d.clock.items():
                    if kn.get(k, -1) < v:
                        kn[k] = v
                if kn.get(ck, -1) < cv:
                    kn[ck] = cv
            clk = dict(kn)
            if o.is_dma:
                clk[("k", o.key)] = o.keycnt
            else:
                clk[o.eng] = o.idx
            o.clock = clk
        for o in self.final_waits:
            o.signal = True
        for e in ENGS:
            c = 0
            for o in self.ops[e]:
                if o.is_dma:
                    o.semval = 16 * o.keycnt
                elif o.signal:
                    c += 1
                    o.semval = c

    def emit(self, block, sem_alloc):
        self.resolve()
        esem = {e: sem_alloc("e_" + e) for e in ENGS}
        ksem = {k: sem_alloc("k_" + str(k)) for k in self.keycnt}

        def run(ename, eng):
            for o in self.ops[ename]:
                for d in o.waits:
                    if d.is_dma:
                        eng.wait_ge(ksem[d.key], d.semval)
                    else:
                        eng.wait_ge(esem[d.eng], d.semval)
                ins = o.fn(eng)
                if o.is_dma:
                    ins.then_inc(ksem[o.key], 16)
                elif o.signal:
                    ins.then_inc(esem[ename], 1)
            if ename == "sp":
                for o in self.final_waits:
                    eng.wait_ge(ksem[o.key], o.semval)

        @block.tensor
        def _(e):
            run("pe", e)

        @block.scalar
        def _(e):
            run("act", e)

        @block.vector
        def _(e):
            run("dve", e)

        @block.gpsimd
        def _(e):
            run("pool", e)

        @block.sync
        def _(e):
            run("sp", e)

    def stats(self):
        return {e: (len(self.ops[e]), sum(len(o.waits) for o in self.ops[e]),
                    sum(1 for o in self.ops[e] if o.signal)) for e in ENGS}


PV = {}
_o = 0
for _n, _w in (("cT", 8), ("ada_b", 192), ("ln_g", 64), ("ln_b", 64), ("dn_conv", 192),
               ("dn_norm", 2), ("dn_alog", 16), ("dn_dtb", 16), ("cf_dw", 496), ("cf_dwb", 16),
               ("cf_lng", 16), ("cf_lnb", 16)):
    PV[_n] = _o
    _o += _w
NPV = _o

CST = {"U": 0, "SL": 128, "MS": 256, "ID": 384, "ONES": 512}
NCST = 640

G_ADA = 0
G_LAYER = [48, 48 + 26, 48 + 26 + 22, 48 + 26 + 22 + 26]
NGROUPS = 48 + 26 + 22 + 26 + 22


def _kmajor(w, n0):
    return np.ascontiguousarray(w[:, n0:n0 + 512].reshape(8, 128, 512).transpose(1, 0, 2)).reshape(128, 4096)


def _w2group(w2, o):
    return np.ascontiguousarray(w2[:, o * 128:(o + 1) * 128].reshape(32, 128, 128).transpose(1, 0, 2)).reshape(128, 4096)


def _fm(v):
    v = np.asarray(v, np.float32).reshape(-1, 128)
    return np.ascontiguousarray(v.T)


def prep_shared(inp):
    f = lambda k: np.asarray(inp[k], np.float32)
    wst = np.empty((NGROUPS, 128, 4096), np.float32)
    ada_w = f("ada_w")
    for l in range(4):
        for j in range(12):
            wst[G_ADA + l * 12 + j] = _kmajor(ada_w[l], j * 512)
    ff1, ff2 = f("ff_w1"), f("ff_w2")
    dn_in, dn_out = f("dn_w_in"), f("dn_w_out")
    cf_in, cf_out = f("cf_w_in"), f("cf_w_out")
    for l in range(4):
        j = l // 2
        g = G_LAYER[l]
        if l % 2 == 0:
            for i in range(8):
                wst[g + i] = _kmajor(dn_in[j], i * 512)
            for i in range(2):
                wst[g + 8 + i] = _kmajor(dn_out[j], i * 512)
            g += 10
        else:
            for i in range(4):
                wsel = np.concatenate([cf_in[j][:, i * 256:(i + 1) * 256], cf_in[j][:, 1024 + i * 256:1024 + (i + 1) * 256]], axis=1)
                wst[g + i] = _kmajor(wsel, 0)
            for i in range(2):
                wst[g + 4 + i] = _kmajor(cf_out[j], i * 512)
            g += 6
        for i in range(8):
            wst[g + i] = _kmajor(ff1[l], i * 512)
        for i in range(8):
            wst[g + 8 + i] = _w2group(ff2[l], i)
    wsm = np.ascontiguousarray(dn_in[:, :, 4096:4112].reshape(2, 8, 128, 16).transpose(2, 0, 1, 3)).reshape(128, 256)
    cst = np.zeros((128, NCST), np.float32)
    i = np.arange(128)
    cst[:, CST["U"]:CST["U"] + 128] = (i[:, None] <= i[None, :])
    cst[:, CST["SL"]:CST["SL"] + 128] = (i[None, :] < i[:, None])
    cst[:, CST["MS"]:CST["MS"] + 128] = (i[:, None] < i[None, :])
    cst[:, CST["ID"]:CST["ID"] + 128] = np.eye(128)
    cst[:, CST["ONES"]:CST["ONES"] + 128] = 1.0
    pv = np.zeros((128, NPV), np.float32)
    pv[:, PV["ada_b"]:PV["ada_b"] + 192] = _fm(f("ada_b"))
    pv[:, PV["ln_g"]:PV["ln_g"] + 64] = _fm(f("ln_g"))
    pv[:, PV["ln_b"]:PV["ln_b"] + 64] = _fm(f("ln_b"))
    cw = f("dn_conv_w")
    t = cw.reshape(2, 4, 24, 128).transpose(3, 0, 2, 1)
    pv[:, PV["dn_conv"]:PV["dn_conv"] + 192] = t.reshape(128, 192)
    pv[:, PV["dn_norm"]:PV["dn_norm"] + 2] = f("dn_norm_w").T
    pv[:, PV["dn_alog"]:PV["dn_alog"] + 16] = np.broadcast_to(f("dn_a_log").reshape(1, 16), (128, 16))
    pv[:, PV["dn_dtb"]:PV["dn_dtb"] + 16] = np.broadcast_to(f("dn_dt_bias").reshape(1, 16), (128, 16))
    dw = f("cf_dw_w")
    t = dw.reshape(2, 31, 8, 128).transpose(3, 0, 2, 1)
    pv[:, PV["cf_dw"]:PV["cf_dw"] + 496] = t.reshape(128, 496)
    pv[:, PV["cf_dwb"]:PV["cf_dwb"] + 16] = _fm(f("cf_dw_b"))
    pv[:, PV["cf_lng"]:PV["cf_lng"] + 16] = _fm(f("cf_ln_g"))
    pv[:, PV["cf_lnb"]:PV["cf_lnb"] + 16] = _fm(f("cf_ln_b"))
    return wst, wsm, cst, pv


def prep_masks():
    i = np.arange(128)
    msk = np.zeros((128, 7, 128), np.float32)
    msk[:, 0, :] = (i[:, None] // 8 == i[None, :] // 8)
    msk[:, 5, :] = np.where(i[:, None] <= i[None, :], 0.0, -30000.0)
    msk[:, 6, :] = np.where(i[:, None] < i[None, :], 0.0, -30000.0)
    for n, m in enumerate((8, 16, 32, 64)):
        for a in range(0, 128, 2 * m):
            msk[a:a + m, n + 1, a + m:a + 2 * m] = -1.0
    return msk.reshape(128, 896)


def prep_core(inp, pv_shared, b):
    x = np.asarray(inp["x"], np.float32)[b]
    xin = np.ascontiguousarray(x.T.reshape(8, 128, 2048).transpose(1, 0, 2))
    pv = pv_shared.copy()
    pv[:, PV["cT"]:PV["cT"] + 8] = _fm(np.asarray(inp["c"], np.float32)[b])
    return xin, pv


DBG = {}


def build(plan, dbg_spec=None):
    nc = bass.Bass("TRN2", target_bir_lowering=False)
    xin = nc.dram_tensor("xin", [128, 8, S], F32, kind="ExternalInput").ap()
    pvd = nc.dram_tensor("pv", [128, NPV], F32, kind="ExternalInput").ap()
    cstd = nc.dram_tensor("cst", [128, NCST], F32, kind="ExternalInput").ap()
    wst = nc.dram_tensor("wst", [NGROUPS, 128, 4096], F32, kind="ExternalInput").ap()
    wsmd = nc.dram_tensor("wsm", [128, 256], F32, kind="ExternalInput").ap()
    mskd = nc.dram_tensor("msk", [128, 896], F32, kind="ExternalInput").ap()
    yout = nc.dram_tensor("yout", [128, 8, S], F32, kind="ExternalOutput").ap()

    P = Prog(nc)
    ARENA = 212480
    with contextlib.ExitStack() as es:
        arena = es.enter_context(nc.sbuf_tensor("arena", [128, ARENA // 2], BF16))
        psum = es.enter_context(nc.psum_tensor("psum", [128, 4096], F32))
        P.register("arena", ARENA)
        P.register("psum", 16384)
        cur = [0]

        def sb(shape, dt, at=None):
            n = int(np.prod(shape))
            ds = _dsize(dt)
            if at is None:
                off = cur[0]
                cur[0] += (n * ds + 31) // 32 * 32
            else:
                off = at
            assert off + n * ds <= ARENA, ("arena overflow", off, n * ds)
            a = arena[:, off // 2:(off + n * ds) // 2]
            if dt != BF16:
                a = a.bitcast(dt)
            if len(shape) == 2:
                a = a.rearrange("p (a b) -> p a b", a=shape[0])
            elif len(shape) == 3:
                a = a.rearrange("p (a b c) -> p a b c", a=shape[0], b=shape[1])
            return a

        def bank(i, n=1):
            return psum[:, i * 512:(i + n) * 512]

        psi = [0, 0]
        ring1 = [list(range(8))]
        ring2 = [[0, 2, 4, 6]]

        def ps1():
            r = ring1[0]
            b = r[psi[0] % len(r)]
            psi[0] += 1
            return bank(b)

        def ps2():
            r = ring2[0]
            b = r[psi[1] % len(r)]
            psi[1] += 1
            return bank(b, 2)

        xT = sb([8, S], F32)
        wslot = [sb([4096], BF16) for _ in range(NSLOT)]
        hbuf = sb([8, TB], BF16)
        big = sb([32, TB], BF16)
        mixo = sb([8, TB], BF16)
        lnf = [sb([TB], F32) for _ in range(6)]
        pv = sb([NPV], F32)
        mod = sb([192], F32)
        modp1 = sb([192], F32)
        modga = sb([192], F32)
        condb = sb([8], BF16)
        ones_div_d = sb([128], BF16)
        cstb = sb([NCST], BF16)
        U_b = cstb[:, CST["U"]:CST["U"] + 128]
        SL_b = cstb[:, CST["SL"]:CST["SL"] + 128]
        MS_b = cstb[:, CST["MS"]:CST["MS"] + 128]
        ident_b = cstb[:, CST["ID"]:CST["ID"] + 128]
        ones_b = sb([128], BF16)
        ones128_b = sb([128], BF16)
        ones_div128_b = sb([128], BF16)
        wsmb = sb([2, 8, 16], BF16)
        G0 = cur[0]
        GSZ = ARENA - G0
        sq = big[:, 0:8, :]
        tb16 = big[:, 8:16, :]

        block = es.enter_context(nc.Block())

        P.dma("sp", pv, pvd, "ld_pv")
        P.dma("pool", cstb, cstd, "ld_cst")
        for kc in range(8):
            P.dma("sp", xT[:, kc, :], xin[:, kc, :], "ld_x%d" % kc)
        P.op("dve", lambda e: e.memset(ones_div_d, 1.0 / 1024.0), [], [ones_div_d])
        P.op("dve", lambda e: e.memset(ones_b, 1.0), [], [ones_b])
        P.op("dve", lambda e: e.memset(ones128_b, 128.0), [], [ones128_b])
        P.op("dve", lambda e: e.memset(ones_div128_b, 1.0 / 128.0), [], [ones_div128_b])
        P.dma("pool", wsmb, wsmd.rearrange("p (j k n) -> p j k n", j=2, k=8), "ld_wsm")
        cTv = pv[:, PV["cT"]:PV["cT"] + 8]
        P.op("act", lambda e: e.activation(out=condb, in_=cTv, func=AF.Silu), [cTv], [condb])

        use_ctr = [0]

        def load(g):
            s = use_ctr[0] % NSLOT
            use_ctr[0] += 1
            P.dma("pool", wslot[s], wst[g], "w%d" % s)
            return wslot[s]

        def mm(out, lhsT, rhs, start, stop):
            P.op("pe", lambda e: e.matmul(out=out, lhsT=lhsT, rhs=rhs, start=start, stop=stop),
                 [lhsT, rhs], [out])

        adab = pv[:, PV["ada_b"]:PV["ada_b"] + 192]
        layers_needed = []
        for (l, k) in plan:
            if l not in layers_needed:
                layers_needed.append(l)
        ada_pending = []

        ada_loaded = []

        def ada_load():
            if ada_pending:
                l_, j_ = ada_pending.pop(0)
                slot = load(G_ADA + l_ * 12 + j_).rearrange("p (k n) -> p k n", k=8)
                ada_loaded.append((l_, j_, slot))

        def ada_compute():
            if not ada_loaded:
                return
            l, j, slot = ada_loaded.pop(0)
            pb = ps1()
            for jj in range(4):
                for kc in range(8):
                    mm(pb[:, jj:jj + 1], slot[:, kc, jj * 128:(jj + 1) * 128], condb[:, kc:kc + 1], kc == 0, kc == 7)
            c0 = l * 48 + j * 4
            P.op("dve", lambda e: e.tensor_tensor(out=mod[:, c0:c0 + 4], in0=pb[:, 0:4], in1=adab[:, c0:c0 + 4], op=ALU.add),
                 [pb[:, 0:4], adab[:, c0:c0 + 4]], [mod[:, c0:c0 + 4]])
            if j == 11:
                ada_finish(l)

        def ada_finish(l):
            sl = slice(l * 48, (l + 1) * 48)
            P.op("dve", lambda e: e.tensor_scalar(out=modp1[:, sl], in0=mod[:, sl], scalar1=1.0, scalar2=None, op0=ALU.add),
                 [mod[:, sl]], [modp1[:, sl]])
            P.op("dve", lambda e: e.tensor_scalar(out=modga[:, sl], in0=mod[:, sl], scalar1=1.0, scalar2=1.0 / ALPHA,
                                                  op0=ALU.add, op1=ALU.mult),
                 [mod[:, sl]], [modga[:, sl]])

        def ada_flush():
            while ada_pending or ada_loaded:
                ada_load()
                ada_compute()

        ada_pending.extend((layers_needed[0], j) for j in range(12))
        ada_flush()

        def mcol(l, m, kc):
            c = l * 48 + m * 8 + kc
            return slice(c, c + 1)

        def modulate(l, w, tb):
            for kc in range(8):
                src = xT[:, kc, tb * TB:(tb + 1) * TB]
                dst = hbuf[:, kc, :]
                sc = modp1[:, mcol(l, 1 + 3 * w, kc)]
                sh = mod[:, mcol(l, 0 + 3 * w, kc)]
                if kc % 2 == 0:
                    P.op("act", lambda e, src=src, dst=dst, sc=sc, sh=sh: e.activation(out=dst, in_=src, func=AF.Identity, scale=sc, bias=sh),
                         [src, sc, sh], [dst])
                else:
                    P.op("dve", lambda e, src=src, dst=dst, sc=sc, sh=sh: e.tensor_scalar(out=dst, in0=src, scalar1=sc, scalar2=sh, op0=ALU.mult, op1=ALU.add),
                         [src, sc, sh], [dst])
            return hbuf

        def residual(l, w, tb, o, yps):
            xs = xT[:, o, tb * TB:(tb + 1) * TB]
            ga = modga[:, mcol(l, 2 + 3 * w, o)]
            P.op("dve", lambda e: e.scalar_tensor_tensor(out=xs, in0=yps, scalar=ga, in1=xs, op0=ALU.mult, op1=ALU.add),
                 [yps, ga, xs], [xs])

        def ln_stats(eps, sq=sq, tb16=tb16):
            m2, var, rstd, mr = lnf[0], lnf[1], lnf[2], lnf[3]
            mean_ps = ps1()
            ex2_ps = ps1()
            for o in range(8):
                mm(mean_ps, ones_div_d, tb16[:, o, :], o == 0, o == 7)
            for o in range(8):
                mm(ex2_ps, ones_div_d, sq[:, o, :], o == 0, o == 7)
            P.op("act", lambda e: e.activation(out=m2, in_=mean_ps, func=AF.Square), [mean_ps], [m2])
            P.op("dve", lambda e: e.scalar_tensor_tensor(out=var, in0=ex2_ps, scalar=eps, in1=m2, op0=ALU.add, op1=ALU.subtract),
                 [ex2_ps, m2], [var])
            P.op("act", lambda e: e.activation(out=m2, in_=var, func=AF.Ln), [var], [m2])
            P.op("act", lambda e: e.activation(out=rstd, in_=m2, func=AF.Exp, scale=-0.5), [m2], [rstd])
            P.op("dve", lambda e: e.tensor_tensor(out=mr, in0=mean_ps, in1=rstd, op=ALU.mult), [mean_ps, rstd], [mr])

        def ln_apply(o, src, apply):
            rstd, mr, n1, n2 = lnf[2], lnf[3], lnf[4], lnf[5]
            P.op("dve", lambda e: e.tensor_tensor(out=n1, in0=src, in1=rstd, op=ALU.mult), [src, rstd], [n1])
            P.op("dve", lambda e: e.tensor_tensor(out=n2, in0=n1, in1=mr, op=ALU.subtract), [n1, mr], [n2])
            apply(o, n2)

        def ln_core(srcs, eps, apply, sq=sq, tb16=tb16):
            ln_stats(eps, sq, tb16)
            for o in range(8):
                ln_apply(o, srcs[o], apply)

        def res_ln_parts(l, w, tb, sq=sq, tb16=tb16):
            srcs = [xT[:, o, tb * TB:(tb + 1) * TB] for o in range(8)]

            def partA():
                for o in range(8):
                    s_ = srcs[o]
                    P.op("act", lambda e, s_=s_, o=o: e.activation(out=sq[:, o, :], in_=s_, func=AF.Square), [s_], [sq[:, o, :]])
                    P.op("dve", lambda e, s_=s_, o=o: e.tensor_copy(out=tb16[:, o, :], in_=s_), [s_], [tb16[:, o, :]])
                ln_stats(LN_EPS / (ALPHA * ALPHA), sq, tb16)

            def apply(o, n2):
                c = (l * 2 + w) * 8 + o
                g = pv[:, PV["ln_g"] + c:PV["ln_g"] + c + 1]
                b = pv[:, PV["ln_b"] + c:PV["ln_b"] + c + 1]
                dst = srcs[o]
                P.op("act", lambda e: e.activation(out=dst, in_=n2, func=AF.Identity, scale=g, bias=b), [n2, g, b], [dst])

            return partA, [(lambda o=o: ln_apply(o, srcs[o], apply)) for o in range(8)]

        def res_ln(l, w, tb, sq=sq, tb16=tb16):
            srcs = [xT[:, o, tb * TB:(tb + 1) * TB] for o in range(8)]
            for o in range(8):
                s_ = srcs[o]
                P.op("act", lambda e, s_=s_, o=o: e.activation(out=sq[:, o, :], in_=s_, func=AF.Square), [s_], [sq[:, o, :]])
                P.op("dve", lambda e, s_=s_, o=o: e.tensor_copy(out=tb16[:, o, :], in_=s_), [s_], [tb16[:, o, :]])

            def apply(o, n2):
                c = (l * 2 + w) * 8 + o
                g = pv[:, PV["ln_g"] + c:PV["ln_g"] + c + 1]
                b = pv[:, PV["ln_b"] + c:PV["ln_b"] + c + 1]
                dst = srcs[o]
                P.op("act", lambda e: e.activation(out=dst, in_=n2, func=AF.Identity, scale=g, bias=b), [n2, g, b], [dst])

            ln_core(srcs, LN_EPS / (ALPHA * ALPHA), apply, sq, tb16)

        def mlp_layer(l):
            gbase = G_LAYER[l] + (10 if l % 2 == 0 else 6)
            sq2 = sb([8, TB], BF16, at=G0)
            tb2 = sb([8, TB], BF16, at=G0 + 8 * TB * 2)
            pend = None
            for tb in range(NB):
                h = modulate(l, 1, tb)
                for g in range(8):
                    slot = load(gbase + g).rearrange("p (k n) -> p k n", k=8)
                    for j in range(4):
                        f = g * 4 + j
                        pst = ps1()
                        for kc in range(8):
                            mm(pst, slot[:, kc, j * 128:(j + 1) * 128], h[:, kc, :], kc == 0, kc == 7)
                        a = big[:, f, :]
                        P.op("act", lambda e, a=a, pst=pst: e.activation(out=a, in_=pst, func=AF.Relu), [pst], [a])
                        P.op("dve", lambda e, a=a: e.tensor_tensor(out=a, in0=a, in1=a, op=ALU.mult), [a], [a])
                    if g == 3 and pend is not None:
                        res_ln(l, 1, pend, sq2, tb2)
                        pend = None
                for o in range(8):
                    slot = load(gbase + 8 + o).rearrange("p (f n) -> p f n", f=32)
                    pst = ps1()
                    for f in range(32):
                        mm(pst, slot[:, f, :], big[:, f, :], f == 0, f == 31)
                    residual(l, 1, tb, o, pst)
                pend = tb
            res_ln(l, 1, pend)

        def cf_layer(l):
            j = l // 2
            off = G0
            ubuf = sb([8, 30 + TB], BF16, at=off); off += 8 * (30 + TB) * 2
            off = (off + 31) // 32 * 32
            cv = sb([8, TB], F32, at=off); off += 8 * TB * 4
            diag = []
            for i in range(2):
                diag.append(sb([31, 128], BF16, at=off)); off += 31 * 128 * 2
            sgt = [sb([TB], F32, at=off), sb([TB], F32, at=off + TB * 4)]
            off += 2 * TB * 4
            assert off <= ARENA
            gb = G_LAYER[l]
            P.op("dve", lambda e: e.memset(ubuf[:, :, 0:30], 0.0), [], [ubuf[:, :, 0:30]])
            dcount = [0]

            def phaseB(tb):
                h = hbuf
                for gi in range(4):
                    sw = load(gb + gi).rearrange("p (k n) -> p k n", k=8)
                    for jj in range(2):
                        ch = gi * 2 + jj
                        vps = ps1()
                        gps = ps1()
                        for kc in range(8):
                            mm(vps, sw[:, kc, jj * 128:(jj + 1) * 128], h[:, kc, :], kc == 0, kc == 7)
                        for kc in range(8):
                            mm(gps, sw[:, kc, 256 + jj * 128:256 + (jj + 1) * 128], h[:, kc, :], kc == 0, kc == 7)
                        st = sgt[ch % 2]
                        P.op("act", lambda e, st=st, gps=gps: e.activation(out=st, in_=gps, func=AF.Sigmoid), [gps], [st])
                        if tb > 0:
                            P.op("dve", lambda e, ch=ch: e.tensor_copy(out=ubuf[:, ch, 0:30], in_=ubuf[:, ch, TB:TB + 30]),
                                 [ubuf[:, ch, TB:TB + 30]], [ubuf[:, ch, 0:30]])
                        ud = ubuf[:, ch, 30:30 + TB]
                        P.op("dve", lambda e, ud=ud, vps=vps, st=st: e.tensor_tensor(out=ud, in0=vps, in1=st, op=ALU.mult), [vps, st], [ud])

            def build_diag(ch):
                dg = diag[dcount[0] % 2]
                dcount[0] += 1
                wv = pv[:, PV["cf_dw"] + (j * 8 + ch) * 31:PV["cf_dw"] + (j * 8 + ch + 1) * 31]
                P.op("dve", lambda e: e.tensor_tensor(
                    out=dg, in0=ident_b.unsqueeze(1).to_broadcast([128, 31, 128]),
                    in1=wv.unsqueeze(2).to_broadcast([128, 31, 128]), op=ALU.mult), [ident_b, wv], [dg])
                return dg

            def cf_apply(o, n2):
                g = pv[:, PV["cf_lng"] + j * 8 + o:PV["cf_lng"] + j * 8 + o + 1]
                b = pv[:, PV["cf_lnb"] + j * 8 + o:PV["cf_lnb"] + j * 8 + o + 1]
                dst = mixo[:, o, :]
                P.op("act", lambda e: e.activation(out=dst, in_=n2, func=AF.Silu, scale=g, bias=b), [n2, g, b], [dst])

            modulate(l, 0, 0)
            phaseB(0)
            if NB > 1:
                modulate(l, 0, 1)
            for tb in range(NB):
                dg = build_diag(0)
                lnB = []
                if tb > 0:
                    pa, lnB = res_ln_parts(l, 0, tb - 1)
                    pa()
                for _ in range(3):
                    ada_load()
                for ch in range(8):
                    cps = ps1()
                    for t in range(31):
                        mm(cps, dg[:, t, :], ubuf[:, ch, t:t + TB], t == 0, t == 30)
                    if ch + 1 < 8:
                        dg = build_diag(ch + 1)
                    bcol = pv[:, PV["cf_dwb"] + j * 8 + ch:PV["cf_dwb"] + j * 8 + ch + 1]
                    cvc = cv[:, ch, :]
                    P.op("act", lambda e, cvc=cvc, cps=cps, bcol=bcol: e.activation(out=cvc, in_=cps, func=AF.Identity, bias=bcol),
                         [cps, bcol], [cvc])
                    P.op("act", lambda e, ch=ch, cps=cps, bcol=bcol: e.activation(out=sq[:, ch, :], in_=cps, func=AF.Square, bias=bcol),
                         [cps, bcol], [sq[:, ch, :]])
                    P.op("dve", lambda e, ch=ch, cvc=cvc: e.tensor_copy(out=tb16[:, ch, :], in_=cvc), [cvc], [tb16[:, ch, :]])
                    if lnB:
                        lnB.pop(0)()
                for _ in range(3):
                    ada_compute()
                ln_stats(LN_EPS)
                for o in range(8):
                    ln_apply(o, cv[:, o, :], cf_apply)
                if tb + 1 < NB:
                    phaseB(tb + 1)
                    if tb + 2 < NB:
                        modulate(l, 0, tb + 2)
                for half in range(2):
                    so = load(gb + 4 + half).rearrange("p (k n) -> p k n", k=8)
                    for jj in range(4):
                        o = half * 4 + jj
                        yps = ps1()
                        for kc in range(8):
                            mm(yps, so[:, kc, jj * 128:(jj + 1) * 128], mixo[:, kc, :], kc == 0, kc == 7)
                        residual(l, 0, tb, o, yps)
            res_ln(l, 0, NB - 1)
            ada_flush()

        def gdn_layer(l):
            j = l // 2
            gb = G_LAYER[l]
            ring1[0] = [0, 1, 2, 3]
            NU = 4
            USZ = 13312
            big_hi = int(big.offset) * 2 + 16 * TB * 2
            ubase = [G0, G0 + USZ, G0 + 2 * USZ, big_hi]
            units = []
            for u in range(NU):
                b0 = ubase[u]
                d = {}
                d["kv"] = sb([1024], BF16, at=b0)
                d["ktok"] = d["kv"][:, 0:512]
                d["vtok"] = d["kv"][:, 512:1024]
                d["r"] = d["ktok"]
                d["kg"] = sb([512], BF16, at=b0 + 2048)
                d["kd"] = sb([512], BF16, at=b0 + 3072)
                d["gu"] = sb([512], BF16, at=b0 + 4096)
                d["eg"] = sb([512], BF16, at=b0 + 5120)
                d["dt"] = sb([512], BF16, at=b0 + 6144)
                d["dti"] = sb([512], BF16, at=b0 + 7168)
                d["dts"] = sb([512], BF16, at=b0 + 8192)
                d["pq"] = [sb([1024], BF16, at=b0 + 9216), sb([1024], BF16, at=b0 + 11264)]
                d["u2"] = d["pq"][0][:, 0:512]
                units.append(d)
            o_ = G0 + 3 * USZ
            vnew = lnf[5].bitcast(BF16)[:, 0:512]
            o2 = lnf[5].bitcast(BF16)[:, 512:1024]
            Sf = sb([8, 128], F32, at=o_); o_ += 4096
            Sb = sb([8, 128], BF16, at=o_); o_ += 2048
            rawb = [sb([528], BF16, at=o_), sb([528], BF16, at=o_ + 1056)]; o_ += 2112
            mskb = sb([7, 128], BF16, at=o_); o_ += 1792
            halo = sb([24, 4], BF16, at=big_hi + USZ)
            d4 = [sb([4, 128], BF16, at=big_hi + USZ + 192), sb([4, 128], BF16, at=big_hi + USZ + 192 + 1024)]
            assert USZ + 192 + 2048 <= 16384
            sc_ = {}
            for nm in ("beta", "negb", "g", "s1", "e2", "gam", "eg", "egl", "ekd", "dd"):
                sc_[nm] = sb([4, 8], F32, at=o_); o_ += 128
            nea = sb([8], F32, at=o_); o_ += 32
            gbf = sb([4, 8], BF16, at=o_); o_ += 64
            assert o_ <= ARENA, ("G overflow", o_ - ARENA)
            tmpf, sdf, rrf, ogf = lnf[0], lnf[1], lnf[2], lnf[3]
            sqn = lnf[4].bitcast(BF16)[:, 0:512]

            DBG.update(Sf=Sf, big=big, mixo=mixo, beta=sc_["beta"], g=sc_["g"], gam=sc_["gam"], egl=sc_["egl"], xT=xT)
            P.op("dve", lambda e: e.memset(Sf, 0.0), [], [Sf])
            P.op("dve", lambda e: e.memset(Sb, 0.0), [], [Sb])
            P.dma("pool", mskb, mskd.rearrange("p (a b) -> p a b", a=7), "ld_msk")
            NEGI_b = mskb[:, 5, :]
            NEGS_b = mskb[:, 6, :]
            BD8_b = mskb[:, 0, :]
            NM_b = {8: mskb[:, 1, :], 16: mskb[:, 2, :], 32: mskb[:, 3, :], 64: mskb[:, 4, :]}
            alog = pv[:, PV["dn_alog"] + j * 8:PV["dn_alog"] + j * 8 + 8]
            dtb = pv[:, PV["dn_dtb"] + j * 8:PV["dn_dtb"] + j * 8 + 8]
            normw = pv[:, PV["dn_norm"] + j:PV["dn_norm"] + j + 1]
            P.op("act", lambda e: e.activation(out=nea, in_=alog, func=AF.Exp), [alog], [nea])
            P.op("dve", lambda e: e.tensor_scalar(out=nea, in0=nea, scalar1=-1.0, scalar2=None, op0=ALU.mult), [nea], [nea])
            Uf_ = U_b
            rawc = [0]
            g_ = sc_["g"]

            def bc3(ap2, n):
                return ap2.unsqueeze(2).to_broadcast([128, ap2.shape[1], n])

            def bcm(ap2, n):
                return ap2.unsqueeze(1).to_broadcast([128, n, ap2.shape[1]])

            def v3(ap):
                return ap.rearrange("p (h c) -> p h c", h=4)

            for tb in range(NB):
                h = modulate(l, 0, tb)
                def scalars(h):
                    ba_ps = ps1()[:, 0:64]
                    for i in range(4):
                        for kc in range(8):
                            mm(ba_ps[:, i * 16:(i + 1) * 16], h[:, kc, i * 128:(i + 1) * 128], wsmb[:, j, kc, :], kc == 0, kc == 7)
                    ba3 = ba_ps.rearrange("p (i n) -> p i n", i=4)
                    btv, atv = ba3[:, :, 0:8], ba3[:, :, 8:16]
                    beta, negb, g_, s1, e2 = sc_["beta"], sc_["negb"], sc_["g"], sc_["s1"], sc_["e2"]
                    P.op("act", lambda e: e.activation(out=beta, in_=btv, func=AF.Exp, scale=-1.0), [ba_ps], [beta])
                    P.op("dve", lambda e: e.tensor_scalar(out=beta, in0=beta, scalar1=1.0, scalar2=None, op0=ALU.add), [beta], [beta])
                    P.op("dve", lambda e: e.reciprocal(out=beta, in_=beta), [beta], [beta])
                    P.op("dve", lambda e: e.tensor_scalar(out=negb, in0=beta, scalar1=-1.0, scalar2=None, op0=ALU.mult), [beta], [negb])
                    P.op("dve", lambda e: e.tensor_tensor(out=s1, in0=atv, in1=dtb.unsqueeze(1).to_broadcast([128, 4, 8]), op=ALU.add), [ba_ps, dtb], [s1])
                    P.op("act", lambda e: e.activation(out=e2, in_=s1, func=AF.Exp), [s1], [e2])
                    P.op("act", lambda e: e.activation(out=e2, in_=e2, func=AF.Ln, bias=1.0), [e2], [e2])
                    P.op("dve", lambda e: e.tensor_tensor(out=g_, in0=e2, in1=nea.unsqueeze(1).to_broadcast([128, 4, 8]), op=ALU.mult), [e2, nea], [g_])
                    g2 = gbf.rearrange("p i n -> p (i n)")
                    P.op("dve", lambda e: e.tensor_copy(out=gbf, in_=g_), [g_], [gbf])
                    gam_ps = ps1()[:, 0:32]
                    gl_ps = ps1()[:, 0:32]
                    mm(gam_ps, U_b, g2, True, True)
                    mm(gl_ps, ones_b, g2, True, True)
                    gam, eg, egl, ekd, dd = sc_["gam"], sc_["eg"], sc_["egl"], sc_["ekd"], sc_["dd"]
                    f2 = lambda a: a.rearrange("p i n -> p (i n)")
                    P.op("act", lambda e: e.activation(out=f2(gam), in_=gam_ps, func=AF.Identity), [gam_ps], [gam])
                    P.op("act", lambda e: e.activation(out=f2(eg), in_=gam_ps, func=AF.Exp), [gam_ps], [eg])
                    P.op("act", lambda e: e.activation(out=f2(egl), in_=gl_ps, func=AF.Exp), [gl_ps], [egl])
                    P.op("dve", lambda e: e.tensor_tensor(out=f2(dd), in0=gl_ps, in1=f2(gam), op=ALU.subtract), [gl_ps, gam], [dd])
                    P.op("act", lambda e: e.activation(out=ekd, in_=dd, func=AF.Exp), [dd], [ekd])
                scalars(h)

                for hg in range(2):
                    HB = 0
                    pendc = []

                    def convB(rb_, dg, dst):
                        cps = ps1()
                        for t in range(4):
                            mm(cps, dg[:, t, :], rb_[:, t:t + TB], t == 0, t == 3)
                        P.op("act", lambda e: e.activation(out=dst, in_=cps, func=AF.Silu), [cps], [dst])

                    for ty in range(3):
                        slot = load(gb + 2 * ty + hg).rearrange("p (k n) -> p k n", k=8)
                        for hh in range(4):
                            ch = ty * 8 + hg * 4 + hh
                            rps = ps1()
                            for kc in range(8):
                                mm(rps, slot[:, kc, hh * 128:(hh + 1) * 128], h[:, kc, :], kc == 0, kc == 7)
                            if pendc:
                                convB(*pendc.pop(0))
                            rb_ = rawb[rawc[0] % 2]
                            dg = d4[rawc[0] % 2]
                            rawc[0] += 1
                            wv = pv[:, PV["dn_conv"] + (j * 24 + ch) * 4:PV["dn_conv"] + (j * 24 + ch) * 4 + 4]
                            P.op("dve", lambda e, dg=dg, wv=wv: e.tensor_tensor(out=dg, in0=bcm(ident_b, 4), in1=bc3(wv, 128), op=ALU.mult),
                                 [ident_b, wv], [dg])
                            P.op("act", lambda e, rb_=rb_, rps=rps: e.activation(out=rb_[:, 3:3 + TB], in_=rps, func=AF.Identity), [rps], [rb_[:, 3:3 + TB]])
                            if tb == 0:
                                P.op("dve", lambda e, rb_=rb_: e.memset(rb_[:, 0:3], 0.0), [], [rb_[:, 0:3]])
                            else:
                                P.op("dve", lambda e, rb_=rb_, ch=ch: e.tensor_copy(out=rb_[:, 0:3], in_=halo[:, ch, 0:3]), [halo[:, ch, 0:3]], [rb_[:, 0:3]])
                            P.op("dve", lambda e, rb_=rb_, ch=ch: e.tensor_copy(out=halo[:, ch, 0:3], in_=rb_[:, TB:TB + 3]), [rb_[:, TB:TB + 3]], [halo[:, ch, 0:3]])
                            pendc.append((rb_, dg, big[:, HB + ty * 4 + hh, :]))
                    slot = load(gb + 6 + hg).rearrange("p (k n) -> p k n", k=8)
                    for hh in range(4):
                        zps = ps1()
                        for kc in range(8):
                            mm(zps, slot[:, kc, hh * 128:(hh + 1) * 128], h[:, kc, :], kc == 0, kc == 7)
                        dst = big[:, HB + 12 + hh, :]
                        P.op("act", lambda e, dst=dst, zps=zps: e.activation(out=dst, in_=zps, func=AF.Silu), [zps], [dst])
                    while pendc:
                        convB(*pendc.pop(0))
                    jobs = [(ty, hh) for ty in range(2) for hh in range(4)]
                    sqn_t = [lnf[4].bitcast(BF16)[:, 0:512], lnf[4].bitcast(BF16)[:, 512:1024],
                             lnf[5].bitcast(BF16)[:, 0:512], lnf[5].bitcast(BF16)[:, 512:1024]]
                    pend = []

                    def l2a(n, ty, hh):
                        src = big[:, HB + ty * 4 + hh, :]
                        sq_ = sqn_t[n % 4]
                        sd_ = lnf[n % 4]
                        P.op("dve", lambda e: e.tensor_tensor(out=sq_, in0=src, in1=src, op=ALU.mult), [src], [sq_])
                        sps = ps1()
                        mm(sps, ones128_b if ty == 0 else ones_b, sq_, True, True)
                        epsv = 128e-6 if ty == 0 else 1e-6
                        P.op("act", lambda e: e.activation(out=sd_, in_=sps, func=AF.Ln, bias=epsv), [sps], [sd_])
                        return (src, sd_)

                    def l2b(src, sd_):
                        P.op("act", lambda e: e.activation(out=sd_, in_=sd_, func=AF.Exp, scale=-0.5), [sd_], [sd_])
                        P.op("dve", lambda e: e.tensor_tensor(out=src, in0=src, in1=sd_, op=ALU.mult), [src, sd_], [src])

                    for n, (ty, hh) in enumerate(jobs):
                        pend.append(l2a(n, ty, hh))
                        if len(pend) > 2:
                            l2b(*pend.pop(0))
                    while pend:
                        l2b(*pend.pop(0))

                    for _ in range(2):
                        ada_load()
                    for pair in range(0, 4, NU):
                        tiles = list(range(pair, pair + NU))

                        def cols(i):
                            return slice(i * 128, (i + 1) * 128)

                        def each(fn):
                            for ui, i in enumerate(tiles):
                                fn(units[ui], i)

                        def st1(d, i):
                            psb = ps1().bitcast(BF16)
                            for hh in range(4):
                                kvw = big[:, HB + 4 + hh, cols(i)]
                                P.op("pe", lambda e, kvw=kvw, hh=hh, psb=psb: e.transpose(out=psb[:, hh * 128:(hh + 1) * 128], in_=kvw, identity=ident_b),
                                     [kvw, ident_b], [psb[:, hh * 128:(hh + 1) * 128]])
                            for hh in range(4):
                                vvw = big[:, HB + 8 + hh, cols(i)]
                                P.op("pe", lambda e, vvw=vvw, hh=hh, psb=psb: e.transpose(out=psb[:, 512 + hh * 128:512 + (hh + 1) * 128], in_=vvw, identity=ident_b),
                                     [vvw, ident_b], [psb[:, 512 + hh * 128:512 + (hh + 1) * 128]])
                            P.op("act", lambda e: e.activation(out=d["kv"], in_=psb, func=AF.Identity), [psb], [d["kv"]])
                            gu3 = v3(d["gu"])
                            gsl = gbf[:, i, 4 * hg:4 * hg + 4]
                            P.op("dve", lambda e: e.tensor_tensor(out=gu3, in0=bcm(Uf_, 4), in1=bc3(gsl, 128), op=ALU.mult), [Uf_, gsl], [d["gu"]])
                        each(st1)

                        def st2(d, i):
                            gr = ps1()
                            mm(gr, ones_b, d["gu"], True, True)
                            P.op("act", lambda e: e.activation(out=d["eg"], in_=gr, func=AF.Exp), [gr], [d["eg"]])
                            dfi = ps1()
                            dfs = ps1()
                            for hh in range(4):
                                sl = slice(hh * 128, (hh + 1) * 128)
                                mm(dfi[:, sl], SL_b, d["gu"][:, sl], True, False)
                                mm(dfi[:, sl], ident_b, NEGI_b, False, True)
                            for hh in range(4):
                                sl = slice(hh * 128, (hh + 1) * 128)
                                mm(dfs[:, sl], SL_b, d["gu"][:, sl], True, False)
                                mm(dfs[:, sl], ident_b, NEGS_b, False, True)
                            P.op("act", lambda e: e.activation(out=d["dti"], in_=dfi, func=AF.Exp), [dfi], [d["dti"]])
                            P.op("act", lambda e: e.activation(out=d["dts"], in_=dfs, func=AF.Exp), [dfs], [d["dts"]])
                            egs = sc_["eg"][:, i, 4 * hg:4 * hg + 4]
                            eks = sc_["ekd"][:, i, 4 * hg:4 * hg + 4]
                            P.op("dve", lambda e: e.tensor_tensor(out=v3(d["kg"]), in0=v3(d["ktok"]), in1=bc3(egs, 128), op=ALU.mult), [d["ktok"], egs], [d["kg"]])
                            P.op("dve", lambda e: e.tensor_tensor(out=v3(d["kd"]), in0=v3(d["ktok"]), in1=bc3(eks, 128), op=ALU.mult), [d["ktok"], eks], [d["kd"]])
                        each(st2)

                        def st3(d, i):
                            bsl = sc_["beta"][:, i, 4 * hg:4 * hg + 4]
                            P.op("dve", lambda e: e.tensor_tensor(out=v3(d["dts"]), in0=v3(d["dts"]), in1=bc3(bsl, 128), op=ALU.mult), [d["dts"], bsl], [d["dts"]])
                            kkp = ps1()
                            qkp = ps1()
                            for hh in range(4):
                                kT = big[:, HB + 4 + hh, cols(i)]
                                mm(kkp[:, hh * 128:(hh + 1) * 128], kT, kT, True, True)
                            for hh in range(4):
                                kT = big[:, HB + 4 + hh, cols(i)]
                                qT = big[:, HB + hh, cols(i)]
                                mm(qkp[:, hh * 128:(hh + 1) * 128], kT, qT, True, True)
                            P0 = d["pq"][0][:, 0:512]
                            P.op("dve", lambda e: e.tensor_tensor(out=P0, in0=kkp, in1=d["dts"], op=ALU.mult), [kkp, d["dts"]], [P0])
                            P.op("dve", lambda e: e.tensor_tensor(out=d["dti"], in0=qkp, in1=d["dti"], op=ALU.mult), [qkp, d["dti"]], [d["dti"]])
                            qv = big[:, HB:HB + 4, cols(i)]
                            P.op("dve", lambda e: e.tensor_tensor(out=v3(d["gu"]), in0=qv, in1=v3(d["eg"]), op=ALU.mult), [qv, d["eg"]], [d["gu"]])
                        each(st3)

                        def st4(d, i):
                            psb = ps1().bitcast(BF16)
                            Bp = d["pq"][0][:, 0:512]
                            Ap = d["pq"][0][:, 512:1024]
                            for hh in range(4):
                                src = Bp[:, hh * 128:(hh + 1) * 128]
                                P.op("pe", lambda e, src=src, hh=hh: e.transpose(out=psb[:, hh * 128:(hh + 1) * 128], in_=src, identity=ident_b),
                                     [src, ident_b], [psb[:, hh * 128:(hh + 1) * 128]])
                            P.op("act", lambda e: e.activation(out=Ap, in_=psb[:, 0:512], func=AF.Identity), [psb[:, 0:512]], [Ap])
                            B0, Q0b = d["dts"], d["dt"]
                            P.op("dve", lambda e: e.tensor_tensor(out=v3(B0), in0=v3(Bp), in1=bcm(BD8_b, 4), op=ALU.mult), [Bp, BD8_b], [B0])
                            P.op("dve", lambda e: e.tensor_tensor(out=v3(Q0b), in0=v3(Ap), in1=bcm(BD8_b, 4), op=ALU.mult), [Ap, BD8_b], [Q0b])
                            P.op("dve", lambda e: e.tensor_tensor(out=v3(d["r"]), in0=bcm(ident_b, 4), in1=v3(B0), op=ALU.subtract), [ident_b, B0], [d["r"]])
                        each(st4)

                        def hs(ap, hh):
                            return ap[:, hh * 128:(hh + 1) * 128]

                        def radd(d, lhs):
                            rp = ps1()
                            for hh in range(4):
                                mm(hs(rp, hh), hs(lhs, hh), hs(d["r"], hh), True, True)
                            P.op("dve", lambda e: e.tensor_tensor(out=d["r"], in0=rp, in1=d["r"], op=ALU.add), [rp, d["r"]], [d["r"]])

                        def b1(d, i):
                            B0, Q0b = d["dts"], d["dt"]
                            pq = ps2()
                            for hh in range(4):
                                mm(pq[:, 512 + hh * 128:512 + (hh + 1) * 128], hs(B0, hh), hs(Q0b, hh), True, True)
                            for hh in range(4):
                                mm(pq[:, hh * 128:(hh + 1) * 128], hs(Q0b, hh), hs(B0, hh), True, True)
                            P.op("act", lambda e: e.activation(out=d["pq"][1], in_=pq, func=AF.Identity), [pq], [d["pq"][1]])
                        each(b1)
                        ring1[0] = list(range(8))
                        each(lambda d, i: radd(d, d["pq"][1][:, 512:1024]))

                        def b2(d, i):
                            P1, Q1, Q2 = d["pq"][1][:, 0:512], d["pq"][1][:, 512:1024], d["eg"]
                            qp = ps1()
                            for hh in range(4):
                                mm(hs(qp, hh), hs(P1, hh), hs(Q1, hh), True, True)
                            P.op("act", lambda e: e.activation(out=Q2, in_=qp, func=AF.Identity), [qp], [Q2])
                        each(b2)
                        each(lambda d, i: radd(d, d["eg"]))

                        for m_ in (8, 16, 32, 64):
                            def mg(d, i, m_=m_):
                                Ap = d["pq"][0][:, 512:1024]
                                X, Tm, tmpb = d["pq"][1][:, 0:512], d["pq"][1][:, 512:1024], d["dts"]
                                psb = ps1().bitcast(BF16)
                                for hh in range(4):
                                    src = hs(d["r"], hh)
                                    P.op("pe", lambda e, src=src, hh=hh: e.transpose(out=psb[:, hh * 128:(hh + 1) * 128], in_=src, identity=ident_b),
                                         [src, ident_b], [psb[:, hh * 128:(hh + 1) * 128]])
                                P.op("act", lambda e: e.activation(out=Tm, in_=psb[:, 0:512], func=AF.Identity), [psb[:, 0:512]], [Tm])
                                xp = ps1()
                                for hh in range(4):
                                    mm(hs(xp, hh), hs(Ap, hh), hs(d["r"], hh), True, True)
                                P.op("act", lambda e: e.activation(out=X, in_=xp, func=AF.Identity), [xp], [X])
                            each(mg)

                            def mgb(d, i, m_=m_):
                                X, Tm, tmpb = d["pq"][1][:, 0:512], d["pq"][1][:, 512:1024], d["dts"]
                                yp = ps1()
                                for hh in range(4):
                                    mm(hs(yp, hh), hs(Tm, hh), hs(X, hh), True, True)
                                nm = NM_b[m_]
                                P.op("dve", lambda e: e.tensor_tensor(out=v3(tmpb), in0=v3(yp), in1=bcm(nm, 4), op=ALU.mult), [yp, nm], [tmpb])
                                P.op("dve", lambda e: e.tensor_tensor(out=d["r"], in0=d["r"], in1=tmpb, op=ALU.add), [d["r"], tmpb], [d["r"]])
                            each(mgb)

                        def st5(d, i):
                            ups = ps1()
                            wps = ps1()
                            for hh in range(4):
                                Rh = d["r"][:, hh * 128:(hh + 1) * 128]
                                mm(ups[:, hh * 128:(hh + 1) * 128], Rh, d["vtok"][:, hh * 128:(hh + 1) * 128], True, True)
                            for hh in range(4):
                                Rh = d["r"][:, hh * 128:(hh + 1) * 128]
                                mm(wps[:, hh * 128:(hh + 1) * 128], d["kg"][:, hh * 128:(hh + 1) * 128], Rh, True, True)
                            bsl = sc_["beta"][:, i, 4 * hg:4 * hg + 4]
                            P.op("dve", lambda e: e.tensor_tensor(out=v3(d["u2"]), in0=v3(ups), in1=bc3(bsl, 128), op=ALU.mult), [ups, bsl], [d["u2"]])
                            wTb = d["kg"]
                            d["wT"] = wTb
                            P.op("act", lambda e: e.activation(out=wTb, in_=wps, func=AF.Identity), [wps], [wTb])
                        each(st5)

                        ring1[0] = list(range(8))
                        Sbh = Sb[:, 4 * hg:4 * hg + 4, :]
                        Sfh = Sf[:, 4 * hg:4 * hg + 4, :]
                        gate_pend = []

                        def gate(d, i, ops_):
                            P.op("act", lambda e: e.activation(out=o2, in_=ops_, func=AF.Square), [ops_], [o2])
                            rps2 = ps1()
                            mm(rps2, ones_div128_b, o2, True, True)
                            P.op("act", lambda e: e.activation(out=sdf, in_=rps2, func=AF.Ln, bias=1e-6), [rps2], [sdf])
                            P.op("act", lambda e: e.activation(out=rrf, in_=sdf, func=AF.Exp, scale=-0.5), [sdf], [rrf])
                            P.op("dve", lambda e: e.tensor_tensor(out=ogf, in0=ops_, in1=rrf, op=ALU.mult), [ops_, rrf], [ogf])
                            zv = big[:, HB + 12:HB + 16, cols(i)]
                            dsto = mixo[:, 4 * hg:4 * hg + 4, cols(i)]
                            P.op("dve", lambda e: e.scalar_tensor_tensor(out=dsto, in0=v3(ogf), scalar=normw, in1=zv, op0=ALU.mult, op1=ALU.mult),
                                 [ogf, normw, zv], [dsto])

                        def state(d, i):
                            Sbh = Sb[:, 4 * hg:4 * hg + 4, :]
                            Sfh = Sf[:, 4 * hg:4 * hg + 4, :]
                            nbs = sc_["negb"][:, i, 4 * hg:4 * hg + 4]
                            egls = sc_["egl"][:, i, 4 * hg:4 * hg + 4]
                            P.op("dve", lambda e: e.tensor_tensor(out=Sfh, in0=Sfh, in1=bc3(egls, 128), op=ALU.mult), [Sfh, egls], [Sfh])
                            wsp = ps1()
                            for hh in range(4):
                                mm(wsp[:, hh * 128:(hh + 1) * 128], d["wT"][:, hh * 128:(hh + 1) * 128], Sbh[:, hh, :], True, True)
                            P.op("dve", lambda e: e.tensor_tensor(out=v3(tmpf), in0=v3(wsp), in1=bc3(nbs, 128), op=ALU.mult), [wsp, nbs], [tmpf])
                            P.op("dve", lambda e: e.tensor_tensor(out=vnew, in0=tmpf, in1=d["u2"], op=ALU.add), [tmpf, d["u2"]], [vnew])
                            ops_ = ps1()
                            for hh in range(4):
                                sl = slice(hh * 128, (hh + 1) * 128)
                                mm(ops_[:, sl], Sbh[:, hh, :], d["gu"][:, sl], True, False)
                                mm(ops_[:, sl], vnew[:, sl], d["dti"][:, sl], False, True)
                            dsp = ps1()
                            for hh in range(4):
                                sl = slice(hh * 128, (hh + 1) * 128)
                                mm(dsp[:, sl], d["kd"][:, sl], vnew[:, sl], True, True)
                            P.op("dve", lambda e: e.tensor_tensor(out=Sbh, in0=Sfh, in1=v3(dsp), op=ALU.add), [Sfh, dsp], [Sbh])
                            P.op("dve", lambda e: e.tensor_tensor(out=Sfh, in0=Sfh, in1=v3(dsp), op=ALU.add), [Sfh, dsp], [Sfh])
                            return ops_

                        for ui, i in enumerate(tiles):
                            ops_ = state(units[ui], i)
                            if gate_pend:
                                gate(*gate_pend.pop(0))
                            gate_pend.append((units[ui], i, ops_))
                        while gate_pend:
                            gate(*gate_pend.pop(0))
                        ring1[0] = [0, 1, 2, 3]
                    for _ in range(2):
                        ada_compute()

                for half in range(2):
                    so = load(gb + 8 + half).rearrange("p (k n) -> p k n", k=8)
                    for jj in range(4):
                        o = half * 4 + jj
                        yps = ps1()
                        for kc in range(8):
                            mm(yps, so[:, kc, jj * 128:(jj + 1) * 128], mixo[:, kc, :], kc == 0, kc == 7)
                        residual(l, 0, tb, o, yps)
                res_ln(l, 0, tb)
            ring1[0] = list(range(8))
            ada_flush()

        for (l, kind) in plan:
            if kind == "mlp":
                mlp_layer(l)
            else:
                nxt = layers_needed.index(l) + 1
                if nxt < len(layers_needed):
                    ada_pending.extend((layers_needed[nxt], j) for j in range(12))
                if l % 2 == 0:
                    gdn_layer(l)
                else:
                    cf_layer(l)

        for tb in range(NB):
            P.dma("sp", yout[:, :, tb * TB:(tb + 1) * TB], xT[:, :, tb * TB:(tb + 1) * TB], "st%d" % tb, final=True)

        def sem_alloc(name):
            return es.enter_context(nc.semaphore(name))

        P.emit(block, sem_alloc)
        print("[build] ops/waits/signals:", P.stats(), "arena used", cur[0], "G", GSZ, flush=True)
    return nc


FULL_PLAN = [(l, k) for l in range(DEPTH) for k in ("mix", "mlp")]


def run_plan(inputs, plan, trace=False):
    wst, wsm, cst, pvs = prep_shared(inputs)
    msk = prep_masks()
    in_maps = []
    for b in range(NCORES):
        xin, pv = prep_core(inputs, pvs, b)
        in_maps.append({"xin": xin, "pv": pv, "cst": cst, "wst": wst, "wsm": wsm, "msk": msk})
    nc = build(plan)
    res = run_bass_kernel_spmd(nc, in_maps, core_ids=list(range(NCORES)), trace=trace)
    out = np.empty((NCORES, S, D), np.float32)
    for b in range(NCORES):
        y = res.results[b]["yout"]
        out[b] = y.transpose(1, 0, 2).reshape(D, S).T
    return out, res


def kernel(**inputs):
    out, _ = run_plan(inputs, FULL_PLAN)
    return out
```
